# Optimizing a Trainium2 kernel written in Bass

```python
import jax, jax.numpy as jnp
from jax import lax
import numpy as np

D_MODEL = 1024
BATCH = 8
SEQ = 4096
DEPTH = 4

GRID_W = 64
CTX_LEN = 256
EPS = 1e-6
EXPAND = 2
D_INNER = EXPAND * D_MODEL
D_CONV = D_INNER // 2
HEAD_DIM = 64
N_Q_HEADS = (D_INNER - D_CONV) // HEAD_DIM
N_KV_HEADS = 4
GQA_GROUP = N_Q_HEADS // N_KV_HEADS
CONF_WIDTH = 31
WINDOW = 128
Q_BLOCK = 128
BAND = Q_BLOCK + 2 * WINDOW
ROPE_THETA = 10000.0
NEG_INF = -1e30
EV_A = 0
EV_Q = 2 * D_CONV
EV_K = EV_Q + N_Q_HEADS * HEAD_DIM
EV_V = EV_K + N_KV_HEADS * HEAD_DIM
EV_G = EV_V + N_KV_HEADS * HEAD_DIM
EV_COLS = EV_G + D_INNER
SC_WIDTH = 3
OD_COLS = 4 * D_INNER
N_EVEN = (DEPTH + 1) // 2
N_ODD = DEPTH // 2

kernel_name = "hybrid_conformer_swa_shortconv_dit"


def rms_norm(x, g):
    xf = x.astype(jnp.float32)
    y = xf * lax.rsqrt(jnp.mean(xf * xf, axis=-1, keepdims=True) + EPS)
    return (y * g.astype(jnp.float32)).astype(x.dtype)


def layer_norm(x, g, b):
    xf = x.astype(jnp.float32)
    mu = jnp.mean(xf, axis=-1, keepdims=True)
    var = jnp.mean(jnp.square(xf - mu), axis=-1, keepdims=True)
    y = (xf - mu) * lax.rsqrt(var + EPS)
    return (y * g.astype(jnp.float32) + b.astype(jnp.float32)).astype(x.dtype)


def adaln(cond, w, b):
    m = jax.nn.silu(cond) @ w + b
    return jnp.split(m, 3, axis=-1)


def modulate(xn, shift, scale):
    return xn * (1 + scale) + shift


def depthwise_conv(u, w, b):
    k = w.shape[0]
    out = lax.conv_general_dilated(
        u, w[:, None, :].astype(u.dtype), window_strides=(1,),
        padding=[(k // 2, k // 2)], dimension_numbers=("NWC", "WIO", "NWC"),
        feature_group_count=u.shape[-1])
    return out + b.astype(u.dtype)


def axial_rope(t, row, col):
    half = t.shape[-1] // 2
    quarter = half // 2
    freqs = ROPE_THETA ** (-jnp.arange(quarter, dtype=jnp.float32) / quarter)

    def rot(u, pos):
        ang = pos[:, None] * freqs[None, :]
        cos = jnp.cos(ang)[:, None, :].astype(u.dtype)
        sin = jnp.sin(ang)[:, None, :].astype(u.dtype)
        u1, u2 = u[..., :quarter], u[..., quarter:]
        return jnp.concatenate([u1 * cos - u2 * sin, u1 * sin + u2 * cos], axis=-1)

    return jnp.concatenate([rot(t[..., :half], row), rot(t[..., half:], col)], axis=-1)


def split_heads(t, n_heads):
    return t.reshape(t.shape[:-1] + (n_heads, HEAD_DIM))


def conformer_conv(a_val, a_gate, dw_w, dw_b, ln_g, ln_b):
    u = a_val * jax.nn.sigmoid(a_gate)
    u = depthwise_conv(u, dw_w, dw_b)
    return jax.nn.silu(layer_norm(u, ln_g, ln_b))


def context_attention(qc, kc, vc, sink):
    bsz, l = qc.shape[:2]
    scale = HEAD_DIM ** -0.5
    s = jnp.einsum("bqhgd,bkhd->bhgqk", qc, kc).astype(jnp.float32) * scale
    sk = jnp.broadcast_to(sink.reshape(N_KV_HEADS, GQA_GROUP)[None, :, :, None, None].astype(jnp.float32),
                          s.shape[:-1] + (1,))
    p = jax.nn.softmax(jnp.concatenate([s, sk], axis=-1), axis=-1)[..., :-1]
    o = jnp.einsum("bhgqk,bkhd->bqhgd", p.astype(vc.dtype), vc)
    return o.reshape(bsz, l, N_Q_HEADS * HEAD_DIM)


def latent_window_attention(q, k, v, kc, vc, sink):
    bsz, s_len = q.shape[:2]
    n_blk = s_len // Q_BLOCK
    n_ctx = kc.shape[1]
    scale = HEAD_DIM ** -0.5
    pad = ((0, 0), (WINDOW, WINDOW), (0, 0), (0, 0))
    kp = jnp.pad(k, pad)
    vp = jnp.pad(v, pad)
    qb = q.reshape(bsz, n_blk, Q_BLOCK, N_KV_HEADS, GQA_GROUP, HEAD_DIM).transpose(1, 0, 2, 3, 4, 5)
    qi = jnp.arange(Q_BLOCK)[:, None]
    kj = jnp.arange(BAND)[None, :]
    rel = kj - qi
    in_window = (rel >= 0) & (rel <= 2 * WINDOW)
    sink_l = sink.reshape(N_KV_HEADS, GQA_GROUP)[None, :, :, None, None].astype(jnp.float32)

    def one_block(args):
        blk, qblk = args
        start = blk * Q_BLOCK
        kb = lax.dynamic_slice_in_dim(kp, start, BAND, axis=1)
        vb = lax.dynamic_slice_in_dim(vp, start, BAND, axis=1)
        j = start + kj - WINDOW
        valid = in_window & (j >= 0) & (j < s_len)
        s_loc = jnp.einsum("bqhgd,bkhd->bhgqk", qblk, kb).astype(jnp.float32) * scale
        s_loc = jnp.where(valid, s_loc, NEG_INF)
        s_ctx = jnp.einsum("bqhgd,bkhd->bhgqk", qblk, kc).astype(jnp.float32) * scale
        sk = jnp.broadcast_to(sink_l, s_ctx.shape[:-1] + (1,))
        p = jax.nn.softmax(jnp.concatenate([s_loc, s_ctx, sk], axis=-1), axis=-1)
        p_loc = p[..., :BAND].astype(v.dtype)
        p_ctx = p[..., BAND:BAND + n_ctx].astype(v.dtype)
        return (jnp.einsum("bhgqk,bkhd->bqhgd", p_loc, vb)
                + jnp.einsum("bhgqk,bkhd->bqhgd", p_ctx, vc))

    o = lax.map(one_block, (jnp.arange(n_blk), qb))
    return o.transpose(1, 0, 2, 3, 4, 5).reshape(bsz, s_len, N_Q_HEADS * HEAD_DIM)


def short_gated_conv(h, conv_w, conv_b, w_out):
    bg = h[..., :D_INNER]
    cg = h[..., D_INNER:2 * D_INNER]
    u = h[..., 2 * D_INNER:3 * D_INNER]
    z = h[..., 3 * D_INNER:]
    y = bg * depthwise_conv(cg * u, conv_w, conv_b)
    return (y * jax.nn.silu(z)) @ w_out


def setup_inputs(seed: int = 0) -> dict:
    key = jax.random.key(seed)
    ks = jax.random.split(key, 24)

    def nrm(k, shape, scale):
        return jax.random.normal(k, shape, jnp.float32) * scale

    return {
        "x": nrm(ks[0], (BATCH, SEQ, D_MODEL), 1.0),
        "c": nrm(ks[1], (BATCH, D_MODEL), 1.0),
        "ctx": nrm(ks[2], (BATCH, CTX_LEN, D_MODEL), 1.0),
        "c_ctx": nrm(ks[3], (D_MODEL,), 1.0),
        "norm_g": 1.0 + nrm(ks[4], (DEPTH, D_MODEL), 0.02),
        "ada_w": nrm(ks[5], (DEPTH, D_MODEL, 3 * D_MODEL), D_MODEL ** -0.5),
        "ada_b": nrm(ks[6], (DEPTH, 3 * D_MODEL), 0.02),
        "ev_w_in": nrm(ks[7], (N_EVEN, D_MODEL, EV_COLS), D_MODEL ** -0.5),
        "ev_dw_w": nrm(ks[8], (N_EVEN, CONF_WIDTH, D_CONV), CONF_WIDTH ** -0.5),
        "ev_dw_b": nrm(ks[9], (N_EVEN, D_CONV), 0.02),
        "ev_ln_g": 1.0 + nrm(ks[10], (N_EVEN, D_CONV), 0.02),
        "ev_ln_b": nrm(ks[11], (N_EVEN, D_CONV), 0.02),
        "ev_sink": nrm(ks[12], (N_EVEN, N_Q_HEADS), 0.5),
        "ev_w_out": nrm(ks[13], (N_EVEN, D_INNER, D_MODEL), D_INNER ** -0.5),
        "od_w_in": nrm(ks[14], (N_ODD, D_MODEL, OD_COLS), D_MODEL ** -0.5),
        "od_conv_w": nrm(ks[15], (N_ODD, SC_WIDTH, D_INNER), SC_WIDTH ** -0.5),
        "od_conv_b": nrm(ks[16], (N_ODD, D_INNER), 0.02),
        "od_w_out": nrm(ks[17], (N_ODD, D_INNER, D_MODEL), D_INNER ** -0.5),
        "final_g": 1.0 + nrm(ks[18], (D_MODEL,), 0.02),
    }


def reference(x, c, ctx, c_ctx, norm_g, ada_w, ada_b, ev_w_in, ev_dw_w, ev_dw_b, ev_ln_g,
              ev_ln_b, ev_sink, ev_w_out, od_w_in, od_conv_w, od_conv_b, od_w_out, final_g):
    bsz, s_len, _ = x.shape
    n_ctx = ctx.shape[1]
    rows_n = s_len // GRID_W
    row = jnp.broadcast_to(jnp.arange(rows_n)[:, None], (rows_n, GRID_W)).reshape(-1).astype(jnp.float32)
    col = jnp.broadcast_to(jnp.arange(GRID_W)[None, :], (rows_n, GRID_W)).reshape(-1).astype(jnp.float32)

    for i in range(DEPTH):
        g = norm_g[i]
        sh, sc, gt = adaln(c, ada_w[i], ada_b[i])
        xn = modulate(rms_norm(x, g), sh[:, None, :], sc[:, None, :])
        ctx_out_needed = any(j % 2 == 0 for j in range(i + 1, DEPTH))

        if i % 2 == 0:
            e = i // 2
            w_in = ev_w_in[e]
            csh, csc, cgt = adaln(c_ctx, ada_w[i], ada_b[i])
            cn = modulate(rms_norm(ctx, g), csh, csc)
            if ctx_out_needed:
                hc = cn @ w_in
                hc_kv = hc[..., EV_K:EV_G]
            else:
                hc_kv = cn @ w_in[:, EV_K:EV_G]
            kc = split_heads(hc_kv[..., :N_KV_HEADS * HEAD_DIM], N_KV_HEADS)
            vc = split_heads(hc_kv[..., N_KV_HEADS * HEAD_DIM:], N_KV_HEADS)

            h = xn @ w_in
            a = conformer_conv(h[..., EV_A:D_CONV], h[..., D_CONV:EV_Q],
                               ev_dw_w[e], ev_dw_b[e], ev_ln_g[e], ev_ln_b[e])
            q = axial_rope(split_heads(h[..., EV_Q:EV_K], N_Q_HEADS), row, col)
            q = q.reshape(bsz, s_len, N_KV_HEADS, GQA_GROUP, HEAD_DIM)
            k = axial_rope(split_heads(h[..., EV_K:EV_V], N_KV_HEADS), row, col)
            v = split_heads(h[..., EV_V:EV_G], N_KV_HEADS)
            att = latent_window_attention(q, k, v, kc, vc, ev_sink[e])
            y = (jnp.concatenate([a, att], axis=-1) * jax.nn.silu(h[..., EV_G:])) @ ev_w_out[e]
            x = x + gt[:, None, :] * y

            if ctx_out_needed:
                a_c = conformer_conv(hc[..., EV_A:D_CONV], hc[..., D_CONV:EV_Q],
                                     ev_dw_w[e], ev_dw_b[e], ev_ln_g[e], ev_ln_b[e])
                qc = split_heads(hc[..., EV_Q:EV_K], N_Q_HEADS).reshape(
                    bsz, n_ctx, N_KV_HEADS, GQA_GROUP, HEAD_DIM)
                att_c = context_attention(qc, kc, vc, ev_sink[e])
                yc = (jnp.concatenate([a_c, att_c], axis=-1) * jax.nn.silu(hc[..., EV_G:])) @ ev_w_out[e]
                ctx = ctx + cgt * yc
        else:
            o = i // 2
            h = xn @ od_w_in[o]
            x = x + gt[:, None, :] * short_gated_conv(h, od_conv_w[o], od_conv_b[o], od_w_out[o])
            if ctx_out_needed:
                csh, csc, cgt = adaln(c_ctx, ada_w[i], ada_b[i])
                cn = modulate(rms_norm(ctx, g), csh, csc)
                hc = cn @ od_w_in[o]
                ctx = ctx + cgt * short_gated_conv(hc, od_conv_w[o], od_conv_b[o], od_w_out[o])

    return rms_norm(x, final_g)
```

```python
import os
import numpy as np
import ml_dtypes
from contextlib import ExitStack
import concourse.bass as bass
import concourse.mybir as mybir
from concourse.bass_utils import run_bass_kernel_spmd

F32 = mybir.dt.float32
BF16 = mybir.dt.bfloat16
AF = mybir.ActivationFunctionType
ALU = mybir.AluOpType
AX = mybir.AxisListType

D = 1024
S_LAT = 4096
N_CTX = 256
NTOK = S_LAT + N_CTX
DEPTH = 4
EPS = 1e-6
EV_COLS = 5632
OD_COLS = 8192
NEG = -30000.0
DEBUG_MAX_PHASE = 10 ** 9
DEBUG_SKIP = set()


class Ins:
    __slots__ = ("eng", "fn", "waits", "is_dma", "sem", "target", "need_sig", "semkey")

    def __init__(self, eng, fn, is_dma, semkey):
        self.eng = eng
        self.fn = fn
        self.waits = []
        self.is_dma = is_dma
        self.sem = None
        self.target = None
        self.need_sig = False
        self.semkey = semkey


class Prog:
    ENGS = ("pe", "act", "dve", "pool", "sp")

    def __init__(self, nc, stack):
        self.nc = nc
        self.stack = stack
        self.esem = {}
        self.ecount = {}
        for e in ("pe", "act", "dve", "pool"):
            self.esem[e] = stack.enter_context(nc.semaphore("S_" + e))
            self.ecount[e] = 0
        self.dsem = {}
        self.dcount = {}
        self.waited = {}
        self.keymap = {}
        self.nflush = 0
        self.reset()

    def reset(self):
        self.lists = {e: [] for e in self.ENGS}
        self.lastw = {}
        self.readers = {}
        self.keymap = {}

    def sk(self, *key):
        if key not in self.keymap:
            self.keymap[key] = "K%d" % len(self.keymap)
        return self.keymap[key]

    def op(self, eng, fn, reads=(), writes=(), dma=False, semkey=None):
        if self.nflush >= DEBUG_MAX_PHASE:
            return None
        ins = Ins(eng, fn, dma, semkey)
        deps = {}
        for b in reads:
            w = self.lastw.get(b)
            if w is not None:
                deps[id(w)] = (w, "raw")
        for b in writes:
            w = self.lastw.get(b)
            if w is not None and id(w) not in deps:
                deps[id(w)] = (w, "waw")
            for r in self.readers.get(b, ()):
                if id(r) not in deps:
                    deps[id(r)] = (r, "war")
        for w, kind in deps.values():
            if w is ins:
                continue
            if not w.is_dma and not ins.is_dma and w.eng == eng:
                if eng == "pe":
                    continue
            ins.waits.append(w)
            w.need_sig = True
        for b in reads:
            self.readers.setdefault(b, []).append(ins)
        for b in writes:
            self.lastw[b] = ins
            self.readers[b] = []
        self.lists[eng].append(ins)
        return ins

    def flush(self):
        nc = self.nc
        self.nflush += 1
        if self.nflush > DEBUG_MAX_PHASE:
            self.reset()
            return
        for e in self.ENGS:
            for ins in self.lists[e]:
                if ins.is_dma:
                    k = ins.semkey
                    if k not in self.dsem:
                        self.dsem[k] = self.stack.enter_context(nc.semaphore("D%d" % len(self.dsem)))
                        self.dcount[k] = 0
                    self.dcount[k] += 16
                    ins.sem = self.dsem[k]
                    ins.target = self.dcount[k]
                elif ins.need_sig:
                    self.ecount[e] += 1
                    ins.sem = self.esem[e]
                    ins.target = self.ecount[e]
        lists = self.lists
        waited = self.waited

        def run(engname, engine):
            dmas = {}
            for ins in lists[engname]:
                need = {}
                for w in ins.waits:
                    key = (engname, id(w.sem))
                    if waited.get(key, 0) < w.target and need.get(key, (None, 0))[1] < w.target:
                        need[key] = (w.sem, w.target)
                need = list(need.items())
                for key, (sem, tgt) in need[:-1]:
                    engine.wait_ge(sem, tgt)
                    waited[key] = tgt
                bi = ins.fn(engine)
                if need:
                    key, (sem, tgt) = need[-1]
                    bi._wait_ge(sem, tgt)
                    waited[key] = tgt
                if ins.is_dma:
                    bi.then_inc(ins.sem, 16)
                    dmas[id(ins.sem)] = (ins.sem, ins.target)
                elif ins.need_sig:
                    bi.then_inc(ins.sem, 1)
            for sem, tgt in dmas.values():
                key = (engname, id(sem))
                if waited.get(key, 0) < tgt:
                    engine.wait_ge(sem, tgt)
                    waited[key] = tgt

        with nc.Block() as block:
            if lists["sp"]:
                @block.sync
                def _(eng):
                    run("sp", eng)
            if lists["act"]:
                @block.scalar
                def _(eng):
                    run("act", eng)
            if lists["pool"]:
                @block.gpsimd
                def _(eng):
                    run("pool", eng)
            if lists["pe"]:
                @block.tensor
                def _(eng):
                    run("pe", eng)
            if lists["dve"]:
                @block.vector
                def _(eng):
                    run("dve", eng)
        self.reset()

    def dma(self, q, out, in_, reads=(), writes=(), semkey=None, **kw):
        return self.op(q, lambda e: e.dma_start(out=out, in_=in_, **kw), reads, writes, dma=True, semkey=semkey)

    def mm(self, out, lhsT, rhs, start, stop, reads=(), writes=()):
        return self.op("pe", lambda e: e.matmul(out, lhsT, rhs, start=start, stop=stop), reads, writes)

    def tr(self, out, in_, ident, reads=(), writes=()):
        return self.op("pe", lambda e: e.transpose(out, in_, ident), reads, writes)

    def act(self, out, in_, func, reads=(), writes=(), **kw):
        return self.op("act", lambda e: e.activation(out=out, in_=in_, func=func, **kw), reads, writes)

    def tt(self, eng, out, in0, in1, op, reads=(), writes=()):
        return self.op(eng, lambda e: e.tensor_tensor(out=out, in0=in0, in1=in1, op=op), reads, writes)

    def ts(self, eng, out, in0, s1, s2, op0, op1=None, reads=(), writes=()):
        if op1 is None:
            return self.op(eng, lambda e: e.tensor_scalar(out=out, in0=in0, scalar1=s1, scalar2=None, op0=op0), reads, writes)
        return self.op(eng, lambda e: e.tensor_scalar(out=out, in0=in0, scalar1=s1, scalar2=s2, op0=op0, op1=op1), reads, writes)

    def stt(self, out, in0, scalar, in1, op0, op1, reads=(), writes=()):
        return self.op("dve", lambda e: e.scalar_tensor_tensor(out=out, in0=in0, scalar=scalar, in1=in1, op0=op0, op1=op1), reads, writes)

    def copy(self, eng, out, in_, reads=(), writes=()):
        if eng == "act":
            return self.op(eng, lambda e: e.copy(out=out, in_=in_), reads, writes)
        return self.op(eng, lambda e: e.tensor_copy(out=out, in_=in_), reads, writes)

    def memset(self, eng, ap, val, writes=()):
        return self.op(eng, lambda e: e.memset(ap, val), (), writes)

    def recip(self, out, in_, reads=(), writes=()):
        return self.op("dve", lambda e: e.reciprocal(out=out, in_=in_), reads, writes)

    def rmax(self, out, in_, reads=(), writes=()):
        return self.op("dve", lambda e: e.reduce_max(out=out, in_=in_, axis=AX.X), reads, writes)


def tiles_for(with_ctx):
    tl = [(512 * i, 512, 0) for i in range(8)]
    if with_ctx:
        tl.append((S_LAT, N_CTX, 1))
    return tl


def seg_bounds(cond):
    return (0, S_LAT) if cond == 0 else (S_LAT, NTOK)


def build_nc(layer_ids):
    nc = bass.Bass("TRN2", target_bir_lowering=False)
    first = layer_ids[0]
    last_is_final = layer_ids[-1] == DEPTH - 1

    def din(name, shape, dt=F32):
        return nc.dram_tensor(name, list(shape), dt, kind="ExternalInput").ap()

    xc = din("xc", [NTOK, D])
    cT = din("cT", [128, 8, 2])
    ada_w = din("ada_w", [DEPTH, D, 3 * D])
    ada_b_bc = din("ada_b_bc", [DEPTH, 128, 3 * D])
    norm_g_bc = din("norm_g_bc", [DEPTH, 128, D])
    ev_w_in = din("ev_w_in", [2, D, EV_COLS])
    ev_w_out = din("ev_w_out", [2, 2048, D])
    od_w_in = din("od_w_in", [2, D, OD_COLS])
    od_w_out = din("od_w_out", [2, 2048, D])
    ev_dww = din("ev_dww", [2, 128, 8, 31])
    ev_vec = din("ev_vec", [2, 128, 3, 8])
    ev_sink_bc = din("ev_sink_bc", [2, 128, 16])
    od_cw = din("od_cw", [2, 128, 16, 3])
    od_cb = din("od_cb", [2, 128, 16])
    fg_bc = din("fg_bc", [128, D])
    ident_d = din("ident", [128, 128], BF16)
    pmat_d = din("pmat", [128, 128], BF16)
    cos_d = din("cos_t", [128, S_LAT])
    sin_d = din("sin_t", [128, S_LAT])
    mask_d = din("mask_b", [128, 384], BF16)

    if last_is_final:
        out_d = nc.dram_tensor("out", [S_LAT, D], F32, kind="ExternalOutput").ap()
        xo_d = None
    else:
        out_d = None
        xo_d = nc.dram_tensor("xo", [NTOK, D], F32, kind="ExternalOutput").ap()
    multi = len(layer_ids) > 1
    X = nc.dram_tensor("Xs", [NTOK, D], F32).ap() if multi else None
    HT = nc.dram_tensor("HT", [4352, NTOK], BF16).ap()
    VT = nc.dram_tensor("VT", [NTOK, 256], BF16).ap()
    GT = nc.dram_tensor("GT", [2048, NTOK], BF16).ap()

    with ExitStack() as gst:
        P = Prog(nc, gst)

        uid = [0]

        def uname(name):
            uid[0] += 1
            return "%s_%d" % (name, uid[0])

        def gsb(name, shape, dt):
            return gst.enter_context(nc.sbuf_tensor(uname(name), list(shape), dt))

        ident = gsb("ident", [128, 128], BF16)
        pmat = gsb("pmat", [128, 128], BF16)
        onesf = gsb("onesf", [128, 128], F32)
        gs_bc = [gsb("gs_bc%d" % w, [128, D], F32) for w in range(2)]
        sh_bc = [gsb("sh_bc%d" % w, [128, D], F32) for w in range(2)]
        gt_bc = [gsb("gt_bc%d" % w, [128, D], F32) for w in range(2)]

        P.dma("sp", ident[:], ident_d, writes=["ident"], semkey=P.sk("ident"))
        P.dma("sp", pmat[:], pmat_d, writes=["pmat"], semkey=P.sk("pmat"))
        P.memset("dve", onesf[:], 1.0, writes=["onesf"])
        for padeng in ("act", "pool", "dve"):
            if ("pad" + padeng) in DEBUG_SKIP:
                padt = gsb("padt" + padeng, [128, 8], F32)
                for _ in range(9000):
                    P.memset(padeng, padt[:], 0.0) if padeng != "act" else P.op("act", lambda e: e.memzero(padt[:]))
        P.flush()

        if "padpe" in DEBUG_SKIP:
            with ExitStack() as ph:
                padp = ph.enter_context(nc.psum_tensor(uname("padp"), [128, 512], F32))
                for _ in range(700):
                    P.mm(padp[:, :128], ident[:], ident[:], True, True)
                P.flush()
        for li, layer in enumerate(layer_ids):
            even = layer % 2 == 0
            e_idx = layer // 2
            ctx_full = layer <= 1
            ctx_kv = layer == 2
            src = xc if li == 0 else X
            is_final = layer == DEPTH - 1
            if is_final:
                dst = None
            elif li == len(layer_ids) - 1:
                dst = xo_d
            else:
                dst = X
            nconds = 2 if (ctx_full or ctx_kv) else 1

            pwc = [0]

            def wload(tile, dst0, wsrc, c0, c1, tag):
                d = dst0
                for cc in range(c0, c1, 512):
                    P.dma("pool", tile[:, :, d:d + 512], wsrc[:, :, cc:cc + 512],
                          writes=[("W", tag, d // 512)], semkey="PW%d" % (pwc[0] % 16))
                    pwc[0] += 1
                    d += 512

            sc_outer = ExitStack()
            sc_inner = ExitStack()
            if even:
                w_in_v = ev_w_in[e_idx].rearrange("(kc p) n -> p kc n", p=128)
                Wp = sc_inner.enter_context(nc.sbuf_tensor(uname("Wp"), [128, 8, EV_COLS], BF16))
            else:
                w_in_v = od_w_in[e_idx].rearrange("(kc p) n -> p kc n", p=128)
                W1a = sc_outer.enter_context(nc.sbuf_tensor(uname("W1a"), [128, 8, 2048], BF16))
                W0 = sc_inner.enter_context(nc.sbuf_tensor(uname("W0"), [128, 8, 4096], BF16))

            with ExitStack() as ph:
                def sb(name, shape, dt):
                    return ph.enter_context(nc.sbuf_tensor(uname(name), list(shape), dt))

                def psm(name, shape, dt):
                    return ph.enter_context(nc.psum_tensor(uname(name), list(shape), dt))
                ct = sb("a_ct", [128, 8, 2], F32)
                sc = sb("a_sc", [128, 8, 2], F32)
                scb = sb("a_scb", [128, 2, 8, 128], F32)
                awt = [sb("a_awt%d" % i, [128, 8, 512], F32) for i in range(2)]
                abb = sb("a_abb", [128, 3 * D], F32)
                ngb = sb("a_ngb", [128, D], F32)
                tmpa = sb("a_tmp", [128, 512], F32)
                pa = [[psm("a_ps%d_%d" % (w, i), [128, 512], F32) for i in range(2)] for w in range(2)]
                P.dma("sp", ct[:], cT, writes=["ct"], semkey=P.sk("ct"))
                P.dma("sp", abb[:], ada_b_bc[layer], writes=["abb"], semkey=P.sk("abb"))
                P.dma("sp", ngb[:], norm_g_bc[layer], writes=["ngb"], semkey=P.sk("ngb"))
                if even:
                    wload(Wp, 0, w_in_v, 0, EV_COLS, "p")
                else:
                    wload(W0, 0, w_in_v, 2048, 6144, "0")
                P.act(sc[:], ct[:], AF.Silu, reads=["ct"], writes=["sc"])
                for w in range(nconds):
                    for kc in range(8):
                        P.ts("dve", scb[:, w, kc, :], onesf[:], sc[:, kc, w:w + 1], None, ALU.mult,
                             reads=["sc", "onesf"], writes=[("scb", w, kc)])
                awv = ada_w[layer].rearrange("(kc p) n -> p kc n", p=128)
                for n in range(6):
                    s = n % 2
                    P.dma("sp", awt[s][:], awv[:, :, n * 512:(n + 1) * 512], writes=[("awt", s)], semkey=P.sk("awt", s))
                    for w in range(nconds):
                        pp = pa[w][s]
                        for kc in range(8):
                            P.mm(pp[:], scb[:, w, kc, :], awt[s][:, kc, :], kc == 0, kc == 7,
                                 reads=[("awt", s), ("scb", w, kc)], writes=[("pa", w, s)])
                        col = (n % 2) * 512
                        bsl = abb[:, n * 512:(n + 1) * 512]
                        if n < 2:
                            P.tt("dve", sh_bc[w][:, col:col + 512], pp[:], bsl, ALU.add,
                                 reads=[("pa", w, s), "abb"], writes=[("sh", w)])
                        elif n < 4:
                            P.stt(tmpa[:], pp[:], 1.0, bsl, ALU.add, ALU.add,
                                  reads=[("pa", w, s), "abb"], writes=["tmpa"])
                            P.tt("dve", gs_bc[w][:, col:col + 512], tmpa[:], ngb[:, col:col + 512], ALU.mult,
                                 reads=["tmpa", "ngb"], writes=[("gs", w)])
                        else:
                            P.tt("dve", gt_bc[w][:, col:col + 512], pp[:], bsl, ALU.add,
                                 reads=[("pa", w, s), "abb"], writes=[("gt", w)])
                P.flush()

            def projection(pass_id, wt, prefetch=None):
                with ExitStack() as ph:
                    def sb(name, shape, dt):
                        return ph.enter_context(nc.sbuf_tensor(uname(name), list(shape), dt))

                    def psm(name, shape, dt):
                        return ph.enter_context(nc.psum_tensor(uname(name), list(shape), dt))
                    if prefetch is not None:
                        prefetch(sb)
                    xt = [sb("p_xt%d" % i, [128, D], F32) for i in range(3)]
                    tmp = [sb("p_tmp%d" % i, [128, D], F32) for i in range(2)]
                    junk = sb("p_junk", [128, D], BF16)
                    xn = sb("p_xn", [128, 4, D], BF16)
                    xnT = [sb("p_xnT%d" % i, [128, 8, 512], BF16) for i in range(2)]
                    ss = [sb("p_ss%d" % i, [128, 4], F32) for i in range(2)]
                    rstd = [sb("p_rstd%d" % i, [128, 4], F32) for i in range(2)]
                    aux = [sb("p_aux%d" % i, [128, 512], F32) for i in range(2)]
                    stg = [sb("p_stg%d" % i, [128, 512], BF16) for i in range(4)]
                    pst = [psm("p_pst%d" % i, [128, 1024], BF16) for i in range(2)]
                    pm = [psm("p_pm%d" % i, [128, 512], F32) for i in range(6)]
                    if even:
                        cs = [sb("p_cos%d" % i, [128, 512], F32) for i in range(2)]
                        sn = [sb("p_sin%d" % i, [128, 512], F32) for i in range(2)]
                        qb = [sb("p_qb%d" % i, [128, 512], BF16) for i in range(2)]
                        t1 = [sb("p_t1%d" % i, [128, 512], F32) for i in range(2)]
                        t2 = [sb("p_t2%d" % i, [128, 512], F32) for i in range(2)]
                        vst = [sb("p_vst%d" % i, [128, 256], BF16) for i in range(2)]
                    cnt = {"xt": 0, "pm": 0, "stg": 0, "aux": 0, "tmp": 0, "rope": 0, "vst": 0, "ev": 0}

                    def nxt(k, n):
                        v = cnt[k] % n
                        cnt[k] += 1
                        return v

                    def evac_engine():
                        return "act" if nxt("ev", 2) == 0 else "dve"

                    tl = tiles_for(ctx_full or ctx_kv)

                    def front_nonpe(ti):
                        t0, T, cond = tl[ti]
                        nsub = T // 128
                        sl = ti % 2
                        if even and cond == 0:
                            P.dma("sp", cs[sl][:, :T], cos_d[:, t0:t0 + T], writes=[("cos", sl)], semkey=P.sk("cos", sl))
                            P.dma("sp", sn[sl][:, :T], sin_d[:, t0:t0 + T], writes=[("sin", sl)], semkey=P.sk("sin", sl))
                        for j in range(nsub):
                            xs = nxt("xt", 3)
                            P.dma("sp", xt[xs][:], src[t0 + j * 128:t0 + (j + 1) * 128, :], writes=[("xt", xs)], semkey=P.sk("xt", xs))
                            P.act(junk[:], xt[xs][:], AF.Square, reads=[("xt", xs)], writes=["junk", ("ss", sl, j)],
                                  accum_out=ss[sl][:, j:j + 1])
                            P.act(rstd[sl][:, j:j + 1], ss[sl][:, j:j + 1], AF.Sqrt, reads=[("ss", sl, j)], writes=[("rstd", sl, j)],
                                  scale=1.0 / D, bias=EPS)
                            P.recip(rstd[sl][:, j:j + 1], rstd[sl][:, j:j + 1], reads=[("rstd", sl, j)], writes=[("rstd", sl, j)])
                            tm = nxt("tmp", 2)
                            P.stt(tmp[tm][:], xt[xs][:], rstd[sl][:, j:j + 1], gs_bc[cond][:], ALU.mult, ALU.mult,
                                  reads=[("xt", xs), ("rstd", sl, j)], writes=[("tmp", tm)])
                            P.tt("pool", xn[:, j, :], tmp[tm][:], sh_bc[cond][:], ALU.add,
                                 reads=[("tmp", tm)], writes=[("xn", j)])

                    def front_pe(ti):
                        t0, T, cond = tl[ti]
                        nsub = T // 128
                        sl = ti % 2
                        for kc in range(8):
                            h = kc % 2
                            for j in range(nsub):
                                P.tr(pst[h][:, j * 128:(j + 1) * 128], xn[:, j, kc * 128:(kc + 1) * 128], ident[:],
                                     reads=[("xn", j)], writes=[("pst", h)])
                            P.copy(evac_engine(), xnT[sl][:, kc, :T], pst[h][:, :T], reads=[("pst", h)], writes=[("xnT", sl, kc)])

                    def jobs(ti):
                        t0, T, cond = tl[ti]
                        nsub = T // 128
                        sl = ti % 2
                        only_kv = even and cond == 1 and ctx_kv

                        def proj(col0, width=128):
                            W, tag, lc = wt(col0)
                            pi = nxt("pm", 6)
                            wk = [("W", tag, i) for i in range(lc // 512, (lc + width - 1) // 512 + 1)]
                            for kc in range(8):
                                P.mm(pm[pi][:width, :T], W[:, kc, lc:lc + width], xnT[sl][:, kc, :T], kc == 0, kc == 7,
                                     reads=wk + [("xnT", sl, kc)], writes=[("pm", pi)])
                            return pi

                        def store(si, row0):
                            P.dma("sp", HT[row0:row0 + 128, t0:t0 + T], stg[si][:, :T], reads=[("stg", si)], semkey=P.sk("stg", si))

                        if even:
                            if not only_kv:
                                for c in range(8):
                                    pg = proj(1024 + c * 128)
                                    ax = nxt("aux", 2)
                                    P.act(aux[ax][:, :T], pm[pg][:, :T], AF.Sigmoid, reads=[("pm", pg)], writes=[("aux", ax)])
                                    pv = proj(c * 128)
                                    si = nxt("stg", 4)
                                    P.tt("dve", stg[si][:, :T], pm[pv][:, :T], aux[ax][:, :T], ALU.mult,
                                         reads=[("pm", pv), ("aux", ax)], writes=[("stg", si)])
                                    store(si, c * 128)
                            for c in range(10):
                                if only_kv and c < 8:
                                    continue
                                pq = proj(2048 + c * 128)
                                si = nxt("stg", 4)
                                if cond == 1:
                                    P.copy(evac_engine(), stg[si][:, :T], pm[pq][:, :T], reads=[("pm", pq)], writes=[("stg", si)])
                                else:
                                    r = nxt("rope", 2)
                                    P.copy("act", qb[r][:, :T], pm[pq][:, :T], reads=[("pm", pq)], writes=[("qb", r)])
                                    p2 = nxt("pm", 6)
                                    P.mm(pm[p2][:, :T], pmat[:], qb[r][:, :T], True, True, reads=[("qb", r)], writes=[("pm", p2)])
                                    P.tt("dve", t1[r][:, :T], pm[pq][:, :T], cs[sl][:, :T], ALU.mult,
                                         reads=[("pm", pq), ("cos", sl), ("qb", r)], writes=[("t1", r)])
                                    P.tt("dve", t2[r][:, :T], pm[p2][:, :T], sn[sl][:, :T], ALU.mult,
                                         reads=[("pm", p2), ("sin", sl)], writes=[("t2", r)])
                                    P.tt("pool", stg[si][:, :T], t1[r][:, :T], t2[r][:, :T], ALU.add,
                                         reads=[("t1", r), ("t2", r)], writes=[("stg", si)])
                                store(si, 1024 + c * 128)
                            Wv, vtag, vlc = wt(3328)
                            for j in range(nsub):
                                pi = nxt("pm", 6)
                                for kc in range(8):
                                    P.mm(pm[pi][:, :256], xnT[sl][:, kc, j * 128:(j + 1) * 128], Wv[:, kc, vlc:vlc + 256], kc == 0, kc == 7,
                                         reads=[("W", vtag, vlc // 512), ("xnT", sl, kc)], writes=[("pm", pi)])
                                vs = nxt("vst", 2)
                                P.copy(evac_engine(), vst[vs][:], pm[pi][:, :256], reads=[("pm", pi)], writes=[("vst", vs)])
                                P.dma("sp", VT[t0 + j * 128:t0 + (j + 1) * 128, :], vst[vs][:], reads=[("vst", vs)], semkey=P.sk("vst", vs))
                            if not only_kv:
                                for c in range(16):
                                    pg = proj(3584 + c * 128)
                                    si = nxt("stg", 4)
                                    P.act(stg[si][:, :T], pm[pg][:, :T], AF.Silu, reads=[("pm", pg)], writes=[("stg", si)])
                                    store(si, 2304 + c * 128)
                        else:
                            for c in range(16):
                                if pass_id == 0:
                                    p1 = proj(2048 + c * 128)
                                    ax = nxt("aux", 2)
                                    P.copy("act", aux[ax][:, :T], pm[p1][:, :T], reads=[("pm", p1)], writes=[("aux", ax)])
                                    p2 = proj(4096 + c * 128)
                                    row0 = c * 128
                                else:
                                    p1 = proj(6144 + c * 128)
                                    ax = nxt("aux", 2)
                                    P.act(aux[ax][:, :T], pm[p1][:, :T], AF.Silu, reads=[("pm", p1)], writes=[("aux", ax)])
                                    p2 = proj(c * 128)
                                    row0 = 2048 + c * 128
                                si = nxt("stg", 4)
                                P.tt("dve", stg[si][:, :T], pm[p2][:, :T], aux[ax][:, :T], ALU.mult,
                                     reads=[("pm", p2), ("aux", ax)], writes=[("stg", si)])
                                store(si, row0)

                    front_nonpe(0)
                    front_pe(0)
                    for ti in range(len(tl)):
                        if ti + 1 < len(tl):
                            front_nonpe(ti + 1)
                        jobs(ti)
                        if ti + 1 < len(tl):
                            front_pe(ti + 1)
                    P.flush()

            if even:
                projection(0, lambda col: (Wp, "p", col))
                sc_inner.close()
            else:
                def wt0(col):
                    return (W0, "0", col - 2048)

                def pre1(sb):
                    wload(W1a, 0, w_in_v, 6144, 7168, "1a")
                    wload(W1a, 1024, w_in_v, 0, 1024, "1a")
                projection(0, wt0, prefetch=pre1)
                sc_inner.close()
                W1b_box = []

                def pre2(sb):
                    W1b = sb("W1b", [128, 8, 2048], BF16)
                    W1b_box.append(W1b)
                    wload(W1b, 0, w_in_v, 7168, 8192, "1b")
                    wload(W1b, 1024, w_in_v, 1024, 2048, "1b")

                def wt1(col):
                    if col >= 6144:
                        c = col - 6144
                        return (W1a, "1a", c) if c < 1024 else (W1b_box[0], "1b", c - 1024)
                    return (W1a, "1a", 1024 + col) if col < 1024 else (W1b_box[0], "1b", 1024 + col - 1024)
                projection(1, wt1, prefetch=pre2)
            sc_outer.close()
            sc_w = ExitStack()
            wo_v = (ev_w_out if even else od_w_out)[e_idx].rearrange("(c p) n -> p c n", p=128)
            Wo = sc_w.enter_context(nc.sbuf_tensor(uname("Wo"), [128, 16, D], BF16))

            def wo_load():
                for cc in range(0, 16, 2):
                    P.dma("pool", Wo[:, cc:cc + 2, :], wo_v[:, cc:cc + 2, :], writes=[("Wo", cc // 2)], semkey="PW%d" % (pwc[0] % 16))
                    pwc[0] += 1

            if even:
                with ExitStack() as ph:
                    def sb(name, shape, dt):
                        return ph.enter_context(nc.sbuf_tensor(uname(name), list(shape), dt))

                    def psm(name, shape, dt):
                        return ph.enter_context(nc.psum_tensor(uname(name), list(shape), dt))
                    dww = sb("c_dww", [128, 8, 31], F32)
                    vec = sb("c_vec", [128, 3, 8], F32)
                    dg = sb("c_dg", [128, 8, 31, 128], BF16)
                    uh = [sb("c_uh%d" % i, [128, 512 + 30], BF16) for i in range(3)]
                    convt = sb("c_conv", [128, 8, 512], F32)
                    sq = [sb("c_sq%d" % i, [128, 512], F32) for i in range(2)]
                    mean = sb("c_mean", [128, 512], F32)
                    msq = sb("c_msq", [128, 512], F32)
                    rs = sb("c_rs", [128, 512], F32)
                    dd = [sb("c_d%d" % i, [128, 512], F32) for i in range(2)]
                    aa = [sb("c_a%d" % i, [128, 512], BF16) for i in range(2)]
                    sg = [sb("c_sg%d" % i, [128, 512], BF16) for i in range(3)]
                    go = [sb("c_go%d" % i, [128, 512], BF16) for i in range(3)]
                    pc = [psm("c_pc%d" % i, [128, 512], F32) for i in range(3)]
                    psum_s = psm("c_pss", [128, 512], F32)
                    psum_q = psm("c_psq", [128, 512], F32)
                    wo_load()
                    P.dma("sp", dww[:], ev_dww[e_idx], writes=["dww"], semkey=P.sk("dww"))
                    P.dma("sp", vec[:], ev_vec[e_idx], writes=["vec"], semkey=P.sk("vec"))
                    for c in range(8):
                        for k in range(31):
                            if (c * 31 + k) % 2 == 0:
                                P.ts("dve", dg[:, c, k, :], ident[:], dww[:, c, k:k + 1], None, ALU.mult,
                                     reads=["dww"], writes=[("dg", c)])
                            else:
                                P.act(dg[:, c, k, :], ident[:], AF.Copy, reads=["dww"], writes=[("dg", c)], scale=dww[:, c, k:k + 1])
                    cn = {"uh": 0, "pc": 0, "sq": 0, "d": 0, "sg": 0}

                    def nx(k, n):
                        v = cn[k] % n
                        cn[k] += 1
                        return v
                    for (t0, T, cond) in tiles_for(ctx_full):
                        lo, hi = seg_bounds(cond)
                        a0 = max(t0 - 15, lo)
                        a1 = min(t0 + T + 15, hi)
                        off = a0 - (t0 - 15)
                        for c in range(8):
                            u = nx("uh", 3)
                            if off > 0 or a1 < t0 + T + 15:
                                P.memset("pool", uh[u][:, :T + 30], 0.0, writes=[("uh", u)])
                            P.dma("sp", uh[u][:, off:off + (a1 - a0)], HT[c * 128:(c + 1) * 128, a0:a1], writes=[("uh", u)], semkey=P.sk("uh", u))
                            pi = nx("pc", 3)
                            for k in range(31):
                                P.mm(pc[pi][:, :T], dg[:, c, k, :], uh[u][:, k:k + T], k == 0, k == 30,
                                     reads=[("dg", c), ("uh", u)], writes=[("pc", pi)])
                            P.act(convt[:, c, :T], pc[pi][:, :T], AF.Identity, reads=[("pc", pi), "vec"], writes=[("conv", c)],
                                  bias=vec[:, 0, c:c + 1])
                            q = nx("sq", 2)
                            P.act(sq[q][:, :T], pc[pi][:, :T], AF.Square, reads=[("pc", pi), "vec"], writes=[("sq", q)],
                                  bias=vec[:, 0, c:c + 1])
                            P.mm(psum_s[:, :T], onesf[:], convt[:, c, :T], c == 0, c == 7, reads=[("conv", c)], writes=["pss"])
                            P.mm(psum_q[:, :T], onesf[:], sq[q][:, :T], c == 0, c == 7, reads=[("sq", q)], writes=["psq"])
                        P.ts("dve", mean[:, :T], psum_s[:, :T], 1.0 / D, None, ALU.mult, reads=["pss"], writes=["mean"])
                        P.tt("dve", msq[:, :T], mean[:, :T], mean[:, :T], ALU.mult, reads=["mean"], writes=["msq"])
                        P.stt(rs[:, :T], psum_q[:, :T], 1.0 / D, msq[:, :T], ALU.mult, ALU.subtract, reads=["psq", "msq"], writes=["rs"])
                        P.act(rs[:, :T], rs[:, :T], AF.Sqrt, reads=["rs"], writes=["rs"], bias=EPS)
                        P.recip(rs[:, :T], rs[:, :T], reads=["rs"], writes=["rs"])
                        for c in range(8):
                            s3 = nx("sg", 3)
                            P.dma("sp", sg[s3][:, :T], HT[2304 + c * 128:2304 + (c + 1) * 128, t0:t0 + T], writes=[("sg", s3)], semkey=P.sk("sg", s3))
                            d = nx("d", 2)
                            P.tt("dve", dd[d][:, :T], convt[:, c, :T], mean[:, :T], ALU.subtract, reads=[("conv", c), "mean"], writes=[("d", d)])
                            P.tt("dve", dd[d][:, :T], dd[d][:, :T], rs[:, :T], ALU.mult, reads=[("d", d), "rs"], writes=[("d", d)])
                            P.act(aa[d][:, :T], dd[d][:, :T], AF.Silu, reads=[("d", d), "vec"], writes=[("a", d)],
                                  scale=vec[:, 1, c:c + 1], bias=vec[:, 2, c:c + 1])
                            P.tt("pool", go[s3][:, :T], aa[d][:, :T], sg[s3][:, :T], ALU.mult, reads=[("a", d), ("sg", s3)], writes=[("go", s3)])
                            P.dma("sp", GT[c * 128:(c + 1) * 128, t0:t0 + T], go[s3][:, :T], reads=[("go", s3)], semkey=P.sk("go", s3))
                    P.flush()
            else:
                with ExitStack() as ph:
                    def sb(name, shape, dt):
                        return ph.enter_context(nc.sbuf_tensor(uname(name), list(shape), dt))

                    def psm(name, shape, dt):
                        return ph.enter_context(nc.psum_tensor(uname(name), list(shape), dt))
                    cw = sb("o_cw", [128, 16, 3], F32)
                    cb = sb("o_cb", [128, 16], F32)
                    dg = sb("o_dg", [128, 16, 3, 128], BF16)
                    NS3 = 6
                    vh = [sb("o_vh%d" % i, [128, 514], BF16) for i in range(NS3)]
                    wt = [sb("o_wt%d" % i, [128, 512], BF16) for i in range(NS3)]
                    go = [sb("o_go%d" % i, [128, 512], BF16) for i in range(NS3)]
                    pc = [psm("o_pc%d" % i, [128, 512], F32) for i in range(NS3)]
                    wo_load()
                    P.dma("sp", cw[:], od_cw[e_idx], writes=["cw"], semkey=P.sk("cw"))
                    P.dma("sp", cb[:], od_cb[e_idx], writes=["cb"], semkey=P.sk("cb"))
                    for c in range(16):
                        for k in range(3):
                            if (c * 3 + k) % 2 == 0:
                                P.ts("dve", dg[:, c, k, :], ident[:], cw[:, c, k:k + 1], None, ALU.mult,
                                     reads=["cw"], writes=[("dg", c)])
                            else:
                                P.act(dg[:, c, k, :], ident[:], AF.Copy, reads=["cw"], writes=[("dg", c)], scale=cw[:, c, k:k + 1])
                    cn = 0
                    for (t0, T, cond) in tiles_for(ctx_full):
                        lo, hi = seg_bounds(cond)
                        a0 = max(t0 - 1, lo)
                        a1 = min(t0 + T + 1, hi)
                        off = a0 - (t0 - 1)
                        for c in range(16):
                            u = cn % NS3
                            cn += 1
                            if off > 0 or a1 < t0 + T + 1:
                                P.memset("pool", vh[u][:, :T + 2], 0.0, writes=[("vh", u)])
                            P.dma("sp", vh[u][:, off:off + (a1 - a0)], HT[c * 128:(c + 1) * 128, a0:a1], writes=[("vh", u)], semkey=P.sk("vh", u))
                            P.dma("sp", wt[u][:, :T], HT[2048 + c * 128:2048 + (c + 1) * 128, t0:t0 + T], writes=[("wt", u)], semkey=P.sk("wt", u))
                            for k in range(3):
                                P.mm(pc[u][:, :T], dg[:, c, k, :], vh[u][:, k:k + T], k == 0, k == 2,
                                     reads=[("dg", c), ("vh", u)], writes=[("pc", u)])
                            P.stt(go[u][:, :T], pc[u][:, :T], cb[:, c:c + 1], wt[u][:, :T], ALU.add, ALU.mult,
                                  reads=[("pc", u), ("wt", u), "cb"], writes=[("go", u)])
                            P.dma("sp", GT[c * 128:(c + 1) * 128, t0:t0 + T], go[u][:, :T], reads=[("go", u)], semkey=P.sk("go", u))
                    P.flush()

            if even and "att" not in DEBUG_SKIP:
                with ExitStack() as ph:
                    def sb(name, shape, dt):
                        return ph.enter_context(nc.sbuf_tensor(uname(name), list(shape), dt))

                    def psm(name, shape, dt):
                        return ph.enter_context(nc.psum_tensor(uname(name), list(shape), dt))
                    kTd = sb("t_kT", [128, 4, NTOK], BF16)
                    Vt = sb("t_V", [128, 34, 256], BF16)
                    maskb = sb("t_mask", [128, 384], BF16)
                    sinkb = sb("t_sink", [128, 16], F32)
                    sink8 = sb("t_sink8", [128, 16], F32)
                    qt = [sb("t_q%d" % i, [128, 8, 128], BF16) for i in range(2)]
                    sgt = [sb("t_sg%d" % i, [128, 8, 128], BF16) for i in range(2)]
                    gsg = [sb("t_go%d" % i, [128, 8, 128], BF16) for i in range(2)]
                    Pt = [sb("t_P%d" % i, [128, 640], BF16) for i in range(2)]
                    PTs = [sb("t_PT%d" % i, [128, 5, 128], BF16) for i in range(2)]
                    dgr = [sb("t_dg%d" % i, [128, 128], BF16) for i in range(2)]
                    st = [sb("t_st%d" % i, [128, 8], F32) for i in range(4)]
                    Sps = [psm("t_S%d" % i, [128, 2, 512], F32) for i in range(2)]
                    PTp = psm("t_PTp", [128, 2, 512], F32)
                    OTp = [psm("t_OT%d" % i, [128, 512], F32) for i in range(2)]
                    for g in range(4):
                        for hb in range(2):
                            P.dma("sp", kTd[hb * 64:(hb + 1) * 64, g, :], HT[2048 + g * 64:2048 + (g + 1) * 64, :],
                                  writes=[("kT", g, hb)], semkey=P.sk("kT", g, hb))
                    P.dma("sp", Vt[:], VT.rearrange("(j p) d -> p j d", p=128), writes=["V"], semkey=P.sk("V"))
                    P.dma("sp", maskb[:], mask_d, writes=["mask"], semkey=P.sk("mask"))
                    P.dma("sp", sinkb[:], ev_sink_bc[e_idx], writes=["sink"], semkey=P.sk("sink"))
                    P.ts("dve", sink8[:], sinkb[:], 8.0, None, ALU.mult, reads=["sink"], writes=["sink8"])
                    nqb = 34 if ctx_full else 32
                    if "attshort" in DEBUG_SKIP:
                        nqb = 2
                    items = []
                    for qb_ in range(nqb):
                        for qc in range(8):
                            for hb in range(2):
                                items.append((qb_, qc, hb))

                    def geom(qb_):
                        if qb_ < 32:
                            kb0 = max(qb_ - 1, 0)
                            kb1 = min(qb_ + 1, 31)
                            nlb = kb1 - kb0 + 1
                            return kb0, nlb, nlb * 128, kb0 * 128, (kb0 - (qb_ - 1)) * 128
                        return 0, 0, 0, 0, 0

                    def stageA(i):
                        qb_, qc, hb = items[i]
                        s2 = qb_ % 2
                        t0 = qb_ * 128
                        if qc == 0 and hb == 0:
                            P.dma("sp", qt[s2][:], HT[1024:2048, t0:t0 + 128].rearrange("(c p) t -> p c t", p=128),
                                  writes=[("q", s2)], semkey=P.sk("q", s2))
                            P.dma("sp", sgt[s2][:], HT[3328:4352, t0:t0 + 128].rearrange("(c p) t -> p c t", p=128),
                                  writes=[("sg", s2)], semkey=P.sk("sg", s2))
                        kb0, nlb, nl, k0, m0 = geom(qb_)
                        h = 2 * qc + hb
                        kvh = h // 4
                        b0 = hb * 64
                        hs = i % 2
                        s4 = i % 4
                        S = Sps[hs]
                        sv = st[s4]
                        ntot = nl + 256
                        q_ap = qt[s2][b0:b0 + 64, qc, :]
                        kc_ap = kTd[b0:b0 + 64, kvh, S_LAT:NTOK]
                        rk = [("q", s2), ("kT", kvh, hb)]
                        if nl == 0:
                            half, nb = 256, 1
                            P.mm(S[:, 0, :256], q_ap, kc_ap, True, True, reads=rk, writes=[("S", hs)])
                        else:
                            half, nb = ntot // 2, 2
                            P.mm(S[:, 0, :half], q_ap, kTd[b0:b0 + 64, kvh, k0:k0 + half], True, False, reads=rk, writes=[("S", hs)])
                            P.mm(S[:, 0, :half], ident[:], maskb[:, m0:m0 + half], False, True, reads=["mask"], writes=[("S", hs)])
                            tail = nl - half
                            if tail > 0:
                                P.mm(S[:, 1, :tail], q_ap, kTd[b0:b0 + 64, kvh, k0 + half:k0 + nl], True, False, reads=rk, writes=[("S", hs)])
                                P.mm(S[:, 1, :tail], ident[:], maskb[:, m0 + half:m0 + nl], False, True, reads=["mask"], writes=[("S", hs)])
                            P.mm(S[:, 1, tail:tail + 256], q_ap, kc_ap, True, True, reads=rk, writes=[("S", hs)])
                        sin_ap = S[:, 0:nb, :half]
                        P.op("dve", lambda e: e.tensor_reduce(out=sv[:, 0:1], in_=sin_ap, axis=AX.XY, op=ALU.max),
                             reads=[("S", hs)], writes=[("mx", s4)])
                        P.ts("dve", sv[:, 3:4], sv[:, 0:1], sink8[:, h:h + 1], -0.125, ALU.max, ALU.mult,
                             reads=[("mx", s4), "sink8"], writes=[("negm", s4)])

                    def stageB(i):
                        qb_, qc, hb = items[i]
                        kb0, nlb, nl, k0, m0 = geom(qb_)
                        h = 2 * qc + hb
                        hs = i % 2
                        s4 = i % 4
                        S = Sps[hs]
                        sv = st[s4]
                        ntot = nl + 256
                        if nl == 0:
                            half, nb = 256, 1
                        else:
                            half, nb = ntot // 2, 2
                        P.act(Pt[hs][:, :ntot].rearrange("p (b k) -> p b k", k=half), S[:, 0:nb, :half], AF.Exp,
                              reads=[("S", hs), ("negm", s4)], writes=[("P", hs), ("rs", s4)],
                              scale=0.125, bias=sv[:, 3:4], accum_out=sv[:, 4:5])
                        P.act(sv[:, 6:7], sinkb[:, h:h + 1], AF.Exp, reads=["sink", ("negm", s4)], writes=[("psk", s4)],
                              bias=sv[:, 3:4])
                        P.ts("dve", sv[:, 7:8], sv[:, 4:5], sv[:, 6:7], None, ALU.add,
                             reads=[("rs", s4), ("psk", s4)], writes=[("rsum", s4)])
                        P.recip(sv[:, 7:8], sv[:, 7:8], reads=[("rsum", s4)], writes=[("rsum", s4)])
                        P.act(dgr[hs][:], ident[:], AF.Copy, reads=[("rsum", s4)], writes=[("dgr", hs)], scale=sv[:, 7:8])

                    def stageC(i):
                        qb_, qc, hb = items[i]
                        s2 = qb_ % 2
                        t0 = qb_ * 128
                        kb0, nlb, nl, k0, m0 = geom(qb_)
                        h = 2 * qc + hb
                        kvh = h // 4
                        b0 = hb * 64
                        hs = i % 2
                        nblk = nlb + 2
                        for blk in range(nblk):
                            P.mm(PTp[:, blk // 4, (blk % 4) * 128:(blk % 4 + 1) * 128], Pt[hs][:, blk * 128:(blk + 1) * 128], dgr[hs][:], True, True,
                                 reads=[("P", hs), ("dgr", hs)], writes=[("PTp", blk // 4)])
                        n0 = min(nblk, 4)
                        P.copy("act", PTs[hs][:, 0:n0, :], PTp[:, 0, :n0 * 128].rearrange("p (b t) -> p b t", t=128),
                               reads=[("PTp", 0)], writes=[("PTs0", hs)])
                        if nblk > 4:
                            P.copy("dve", PTs[hs][:, 4, :], PTp[:, 1, :128], reads=[("PTp", 1)], writes=[("PTs1", hs)])
                        osl = qc % 2
                        for blk in range(nblk):
                            kt = (kb0 + blk) if blk < nlb else (32 + blk - nlb)
                            P.mm(OTp[osl][b0:b0 + 64, :128], Vt[:, kt, kvh * 64:(kvh + 1) * 64], PTs[hs][:, blk, :], blk == 0, blk == nblk - 1,
                                 reads=["V", ("PTs0", hs)] + ([("PTs1", hs)] if blk >= 4 else []), writes=[("OT", osl, hb)])
                        if hb == 1:
                            P.tt("dve", gsg[s2][:, qc, :], OTp[osl][:, :128], sgt[s2][:, qc, :], ALU.mult,
                                 reads=[("OT", osl, 0), ("OT", osl, 1), ("sg", s2)], writes=[("gsg", s2)])
                            if qc == 7:
                                P.dma("sp", GT[1024:2048, t0:t0 + 128].rearrange("(c p) t -> p c t", p=128), gsg[s2][:],
                                      reads=[("gsg", s2)], semkey=P.sk("gsg", s2))

                    n_it = len(items)
                    for step in range(n_it + 2):
                        if step < n_it:
                            stageA(step)
                        if 0 <= step - 1 < n_it:
                            stageB(step - 1)
                        if 0 <= step - 2 < n_it:
                            stageC(step - 2)
                    P.flush()

            with ExitStack() as ph:
                def sb(name, shape, dt):
                    return ph.enter_context(nc.sbuf_tensor(uname(name), list(shape), dt))

                def psm(name, shape, dt):
                    return ph.enter_context(nc.psum_tensor(uname(name), list(shape), dt))
                gtt = [sb("y_g%d" % i, [128, 16, 512], BF16) for i in range(2)]
                xt = [sb("y_xt%d" % i, [128, D], F32) for i in range(3)]
                tmp = [sb("y_tmp%d" % i, [128, D], F32) for i in range(2)]
                xo = [sb("y_xo%d" % i, [128, D], F32) for i in range(3)]
                py = [psm("y_p%d" % i, [128, 512], F32) for i in range(4)]
                if is_final:
                    fg = sb("y_fg", [128, D], F32)
                    junk = sb("y_junk", [128, D], BF16)
                    sv = [sb("y_sv%d" % i, [128, 2], F32) for i in range(3)]
                    xf = [sb("y_xf%d" % i, [128, D], F32) for i in range(3)]
                    P.dma("sp", fg[:], fg_bc, writes=["fg"], semkey=P.sk("fg"))
                cj = 0
                for ti, (t0, T, cond) in enumerate(tiles_for(ctx_full)):
                    g2 = ti % 2
                    P.dma("sp", gtt[g2][:, :, :T], GT[:, t0:t0 + T].rearrange("(c p) t -> p c t", p=128),
                          writes=[("gtt", g2)], semkey=P.sk("gtt", g2))
                    for j in range(T // 128):
                        x3 = cj % 3
                        x2 = cj % 2
                        r0 = t0 + j * 128
                        P.dma("sp", xt[x3][:], src[r0:r0 + 128, :], writes=[("xt", x3)], semkey=P.sk("xt", x3))
                        for nh in range(2):
                            pp = (cj * 2 + nh) % 4
                            for c in range(16):
                                P.mm(py[pp][:], gtt[g2][:, c, j * 128:(j + 1) * 128], Wo[:, c, nh * 512:(nh + 1) * 512], c == 0, c == 15,
                                     reads=[("gtt", g2), ("Wo", c // 2)], writes=[("py", pp)])
                            P.tt("dve", tmp[x2][:, nh * 512:(nh + 1) * 512], py[pp][:], gt_bc[cond][:, nh * 512:(nh + 1) * 512], ALU.mult,
                                 reads=[("py", pp)], writes=[("tmp", x2, nh)])
                        P.tt("pool", xo[x3][:], xt[x3][:], tmp[x2][:], ALU.add,
                             reads=[("xt", x3), ("tmp", x2, 0), ("tmp", x2, 1)], writes=[("xo", x3)])
                        if not is_final:
                            P.dma("sp", dst[r0:r0 + 128, :], xo[x3][:], reads=[("xo", x3)], semkey=P.sk("xo", x3))
                        else:
                            P.act(junk[:], xo[x3][:], AF.Square, reads=[("xo", x3)], writes=["junk", ("fss", x3)], accum_out=sv[x3][:, 0:1])
                            P.act(sv[x3][:, 1:2], sv[x3][:, 0:1], AF.Sqrt, reads=[("fss", x3)], writes=[("frs", x3)], scale=1.0 / D, bias=EPS)
                            P.recip(sv[x3][:, 1:2], sv[x3][:, 1:2], reads=[("frs", x3)], writes=[("frs", x3)])
                            P.stt(xf[x3][:], xo[x3][:], sv[x3][:, 1:2], fg[:], ALU.mult, ALU.mult,
                                  reads=[("xo", x3), ("frs", x3), "fg"], writes=[("xf", x3)])
                            P.dma("sp", out_d[r0:r0 + 128, :], xf[x3][:], reads=[("xf", x3)], semkey=P.sk("xf", x3))
                        cj += 1
                P.flush()

            sc_w.close()
            if (not is_final) and li == len(layer_ids) - 1 and not ctx_full:
                with ExitStack() as ph:
                    cpt = ph.enter_context(nc.sbuf_tensor(uname("cp_t"), [128, 2, D], F32))
                    P.dma("sp", cpt[:], src[S_LAT:NTOK, :].rearrange("(j p) n -> p j n", p=128), writes=["cp"], semkey=P.sk("cp"))
                    P.dma("sp", dst[S_LAT:NTOK, :].rearrange("(j p) n -> p j n", p=128), cpt[:], reads=["cp"], semkey=P.sk("cp2"))
                    P.flush()
    return nc


def _consts():
    ident = np.eye(128, dtype=np.float32).astype(ml_dtypes.bfloat16)
    pm = np.zeros((128, 128), np.float32)
    for m in range(128):
        d = m % 64
        partner = m + 16 if (d % 32) < 16 else m - 16
        pm[partner, m] = 1.0
    pm = pm.astype(ml_dtypes.bfloat16)
    t = np.arange(S_LAT)
    row = (t // 64).astype(np.float32)
    col = (t % 64).astype(np.float32)
    freqs = (10000.0 ** (-np.arange(16, dtype=np.float32) / 16)).astype(np.float32)
    cos_t = np.zeros((128, S_LAT), np.float32)
    sin_t = np.zeros((128, S_LAT), np.float32)
    for p in range(128):
        d = p % 64
        pos = row if d < 32 else col
        f = freqs[d % 16]
        ang = (pos * f).astype(np.float32)
        cos_t[p] = np.cos(ang)
        s = np.sin(ang)
        sin_t[p] = -s if (d % 32) < 16 else s
    qi = np.arange(128)[:, None]
    kj = np.arange(384)[None, :]
    rel = kj - qi
    mask = np.where((rel >= 0) & (rel <= 256), 0.0, NEG).astype(np.float32).astype(ml_dtypes.bfloat16)
    return ident, pm, cos_t, sin_t, mask


def _prep_shared(inp):
    f = lambda a: np.ascontiguousarray(np.asarray(a, dtype=np.float32))
    ident, pm, cos_t, sin_t, mask = _consts()
    sh = {}
    sh["ada_w"] = f(inp["ada_w"])
    sh["ada_b_bc"] = f(np.broadcast_to(f(inp["ada_b"])[:, None, :], (DEPTH, 128, 3 * D)))
    sh["norm_g_bc"] = f(np.broadcast_to(f(inp["norm_g"])[:, None, :], (DEPTH, 128, D)))
    sh["ev_w_in"] = f(inp["ev_w_in"])
    sh["ev_w_out"] = f(inp["ev_w_out"])
    sh["od_w_in"] = f(inp["od_w_in"])
    sh["od_w_out"] = f(inp["od_w_out"])
    sh["ev_dww"] = f(f(inp["ev_dw_w"]).reshape(2, 31, 8, 128).transpose(0, 3, 2, 1))
    vec = np.stack([f(inp["ev_dw_b"]), f(inp["ev_ln_g"]), f(inp["ev_ln_b"])], axis=1)
    sh["ev_vec"] = f(vec.reshape(2, 3, 8, 128).transpose(0, 3, 1, 2))
    sh["ev_sink_bc"] = f(np.broadcast_to(f(inp["ev_sink"])[:, None, :], (2, 128, 16)))
    sh["od_cw"] = f(f(inp["od_conv_w"]).reshape(2, 3, 16, 128).transpose(0, 3, 2, 1))
    sh["od_cb"] = f(f(inp["od_conv_b"]).reshape(2, 16, 128).transpose(0, 2, 1))
    sh["fg_bc"] = f(np.broadcast_to(f(inp["final_g"])[None, :], (128, D)))
    sh["ident"] = ident
    sh["pmat"] = pm
    sh["cos_t"] = cos_t
    sh["sin_t"] = sin_t
    sh["mask_b"] = mask
    return sh


def _cT(c_b, c_ctx):
    a = np.stack([np.asarray(c_b, np.float32), np.asarray(c_ctx, np.float32)], axis=-1)
    return np.ascontiguousarray(a.reshape(8, 128, 2).transpose(1, 0, 2))


_NC_CACHE = {}


def _get_nc(layer_ids):
    key = tuple(layer_ids)
    if key not in _NC_CACHE:
        _NC_CACHE[key] = build_nc(list(layer_ids))
    return _NC_CACHE[key]


LAUNCH_PLAN = [[0], [1], [2], [3]]


def kernel(x, c, ctx, c_ctx, norm_g, ada_w, ada_b, ev_w_in, ev_dw_w, ev_dw_b, ev_ln_g, ev_ln_b,
           ev_sink, ev_w_out, od_w_in, od_conv_w, od_conv_b, od_w_out, final_g):
    inp = dict(norm_g=norm_g, ada_w=ada_w, ada_b=ada_b, ev_w_in=ev_w_in, ev_dw_w=ev_dw_w, ev_dw_b=ev_dw_b,
               ev_ln_g=ev_ln_g, ev_ln_b=ev_ln_b, ev_sink=ev_sink, ev_w_out=ev_w_out, od_w_in=od_w_in,
               od_conv_w=od_conv_w, od_conv_b=od_conv_b, od_w_out=od_w_out, final_g=final_g)
    shared = _prep_shared(inp)
    x = np.asarray(x, np.float32)
    ctx = np.asarray(ctx, np.float32)
    c = np.asarray(c, np.float32)
    B = x.shape[0]
    xcs = [np.ascontiguousarray(np.concatenate([x[b], ctx[b]], axis=0)) for b in range(B)]
    cTs = [_cT(c[b], c_ctx) for b in range(B)]
    out = None
    plan = LAUNCH_PLAN
    if os.environ.get("KERNEL_PROBE_PLAN"):
        plan = [[int(c) for c in grp] for grp in os.environ["KERNEL_PROBE_PLAN"].split(",")]
    for layer_ids in plan:
        nc = _get_nc(layer_ids)
        in_maps = []
        for b in range(B):
            m = dict(shared)
            m["xc"] = xcs[b]
            m["cT"] = cTs[b]
            in_maps.append(m)
        res = run_bass_kernel_spmd(nc, in_maps, core_ids=list(range(B)))
        if layer_ids[-1] == DEPTH - 1:
            out = np.stack([np.asarray(res.results[b]["out"], np.float32) for b in range(B)], axis=0)
        else:
            xcs = [np.ascontiguousarray(np.asarray(res.results[b]["xo"], np.float32)) for b in range(B)]
    if out is None:
        if not os.environ.get("KERNEL_PROBE_PLAN"):
            raise RuntimeError("launch plan did not produce the final output")
        out = np.zeros((B, S_LAT, D), np.float32)
    return out
```

```python
import os
import numpy as np
import ml_dtypes
from contextlib import ExitStack
import concourse.bass as bass
import concourse.mybir as mybir
from concourse.bass_utils import run_bass_kernel_spmd

F32 = mybir.dt.float32
BF16 = mybir.dt.bfloat16
AF = mybir.ActivationFunctionType
ALU = mybir.AluOpType
AX = mybir.AxisListType

D = 1024
S_LAT = 4096
N_CTX = 256
NTOK = S_LAT + N_CTX
DEPTH = 4
EPS = 1e-6
EV_COLS = 5632
OD_COLS = 8192
NEG = -30000.0
DEBUG_MAX_PHASE = 10 ** 9
DEBUG_SKIP = set()


class Ins:
    __slots__ = ("eng", "fn", "waits", "is_dma", "sem", "target", "need_sig", "semkey")

    def __init__(self, eng, fn, is_dma, semkey):
        self.eng = eng
        self.fn = fn
        self.waits = []
        self.is_dma = is_dma
        self.sem = None
        self.target = None
        self.need_sig = False
        self.semkey = semkey


class Prog:
    ENGS = ("pe", "act", "dve", "pool", "sp")

    def __init__(self, nc, stack):
        self.nc = nc
        self.stack = stack
        self.esem = {}
        self.ecount = {}
        for e in ("pe", "act", "dve", "pool"):
            self.esem[e] = stack.enter_context(nc.semaphore("S_" + e))
            self.ecount[e] = 0
        self.dsem = {}
        self.dcount = {}
        self.waited = {}
        self.keymap = {}
        self.nflush = 0
        self.reset()

    def reset(self):
        self.lists = {e: [] for e in self.ENGS}
        self.lastw = {}
        self.readers = {}
        self.keymap = {}

    def sk(self, *key):
        if key not in self.keymap:
            self.keymap[key] = "K%d" % len(self.keymap)
        return self.keymap[key]

    def op(self, eng, fn, reads=(), writes=(), dma=False, semkey=None):
        if self.nflush >= DEBUG_MAX_PHASE:
            return None
        ins = Ins(eng, fn, dma, semkey)
        deps = {}
        for b in reads:
            w = self.lastw.get(b)
            if w is not None:
                deps[id(w)] = (w, "raw")
        for b in writes:
            w = self.lastw.get(b)
            if w is not None and id(w) not in deps:
                deps[id(w)] = (w, "waw")
            for r in self.readers.get(b, ()):
                if id(r) not in deps:
                    deps[id(r)] = (r, "war")
        for w, kind in deps.values():
            if w is ins:
                continue
            if not w.is_dma and not ins.is_dma and w.eng == eng:
                if eng == "pe":
                    continue
            ins.waits.append(w)
            w.need_sig = True
        for b in reads:
            self.readers.setdefault(b, []).append(ins)
        for b in writes:
            self.lastw[b] = ins
            self.readers[b] = []
        self.lists[eng].append(ins)
        return ins

    def flush(self):
        nc = self.nc
        self.nflush += 1
        if self.nflush > DEBUG_MAX_PHASE:
            self.reset()
            return
        for e in self.ENGS:
            for ins in self.lists[e]:
                if ins.is_dma:
                    k = ins.semkey
                    if k not in self.dsem:
                        self.dsem[k] = self.stack.enter_context(nc.semaphore("D%d" % len(self.dsem)))
                        self.dcount[k] = 0
                    self.dcount[k] += 16
                    ins.sem = self.dsem[k]
                    ins.target = self.dcount[k]
                elif ins.need_sig:
                    self.ecount[e] += 1
                    ins.sem = self.esem[e]
                    ins.target = self.ecount[e]
        lists = self.lists
        waited = self.waited

        def run(engname, engine):
            dmas = {}
            for ins in lists[engname]:
                need = {}
                for w in ins.waits:
                    key = (engname, id(w.sem))
                    if waited.get(key, 0) < w.target and need.get(key, (None, 0))[1] < w.target:
                        need[key] = (w.sem, w.target)
                need = list(need.items())
                for key, (sem, tgt) in need[:-1]:
                    engine.wait_ge(sem, tgt)
                    waited[key] = tgt
                bi = ins.fn(engine)
                if need:
                    key, (sem, tgt) = need[-1]
                    bi._wait_ge(sem, tgt)
                    waited[key] = tgt
                if ins.is_dma:
                    bi.then_inc(ins.sem, 16)
                    dmas[id(ins.sem)] = (ins.sem, ins.target)
                elif ins.need_sig:
                    bi.then_inc(ins.sem, 1)
            for sem, tgt in dmas.values():
                key = (engname, id(sem))
                if waited.get(key, 0) < tgt:
                    engine.wait_ge(sem, tgt)
                    waited[key] = tgt

        with nc.Block() as block:
            if lists["sp"]:
                @block.sync
                def _(eng):
                    run("sp", eng)
            if lists["act"]:
                @block.scalar
                def _(eng):
                    run("act", eng)
            if lists["pool"]:
                @block.gpsimd
                def _(eng):
                    run("pool", eng)
            if lists["pe"]:
                @block.tensor
                def _(eng):
                    run("pe", eng)
            if lists["dve"]:
                @block.vector
                def _(eng):
                    run("dve", eng)
        self.reset()

    def dma(self, q, out, in_, reads=(), writes=(), semkey=None, **kw):
        return self.op(q, lambda e: e.dma_start(out=out, in_=in_, **kw), reads, writes, dma=True, semkey=semkey)

    def mm(self, out, lhsT, rhs, start, stop, reads=(), writes=()):
        return self.op("pe", lambda e: e.matmul(out, lhsT, rhs, start=start, stop=stop), reads, writes)

    def tr(self, out, in_, ident, reads=(), writes=()):
        return self.op("pe", lambda e: e.transpose(out, in_, ident), reads, writes)

    def act(self, out, in_, func, reads=(), writes=(), **kw):
        return self.op("act", lambda e: e.activation(out=out, in_=in_, func=func, **kw), reads, writes)

    def tt(self, eng, out, in0, in1, op, reads=(), writes=()):
        return self.op(eng, lambda e: e.tensor_tensor(out=out, in0=in0, in1=in1, op=op), reads, writes)

    def ts(self, eng, out, in0, s1, s2, op0, op1=None, reads=(), writes=()):
        if op1 is None:
            return self.op(eng, lambda e: e.tensor_scalar(out=out, in0=in0, scalar1=s1, scalar2=None, op0=op0), reads, writes)
        return self.op(eng, lambda e: e.tensor_scalar(out=out, in0=in0, scalar1=s1, scalar2=s2, op0=op0, op1=op1), reads, writes)

    def stt(self, out, in0, scalar, in1, op0, op1, reads=(), writes=()):
        return self.op("dve", lambda e: e.scalar_tensor_tensor(out=out, in0=in0, scalar=scalar, in1=in1, op0=op0, op1=op1), reads, writes)

    def copy(self, eng, out, in_, reads=(), writes=()):
        if eng == "act":
            return self.op(eng, lambda e: e.copy(out=out, in_=in_), reads, writes)
        return self.op(eng, lambda e: e.tensor_copy(out=out, in_=in_), reads, writes)

    def memset(self, eng, ap, val, writes=()):
        return self.op(eng, lambda e: e.memset(ap, val), (), writes)

    def recip(self, out, in_, reads=(), writes=()):
        return self.op("dve", lambda e: e.reciprocal(out=out, in_=in_), reads, writes)

    def rmax(self, out, in_, reads=(), writes=()):
        return self.op("dve", lambda e: e.reduce_max(out=out, in_=in_, axis=AX.X), reads, writes)


def tiles_for(with_ctx):
    tl = [(512 * i, 512, 0) for i in range(8)]
    if with_ctx:
        tl.append((S_LAT, N_CTX, 1))
    return tl


def seg_bounds(cond):
    return (0, S_LAT) if cond == 0 else (S_LAT, NTOK)


def build_nc(layer_ids):
    nc = bass.Bass("TRN2", target_bir_lowering=False)
    first = layer_ids[0]
    last_is_final = layer_ids[-1] == DEPTH - 1

    def din(name, shape, dt=F32):
        return nc.dram_tensor(name, list(shape), dt, kind="ExternalInput").ap()

    xc = din("xc", [NTOK, D])
    cT = din("cT", [128, 8, 2])
    ada_w = din("ada_w", [DEPTH, D, 3 * D])
    ada_b_bc = din("ada_b_bc", [DEPTH, 128, 3 * D])
    norm_g_bc = din("norm_g_bc", [DEPTH, 128, D])
    ev_w_in = din("ev_w_in", [2, D, EV_COLS])
    ev_w_out = din("ev_w_out", [2, 2048, D])
    od_w_in = din("od_w_in", [2, D, OD_COLS])
    od_w_out = din("od_w_out", [2, 2048, D])
    ev_dww = din("ev_dww", [2, 128, 8, 31])
    ev_vec = din("ev_vec", [2, 128, 3, 8])
    ev_sink_bc = din("ev_sink_bc", [2, 128, 16])
    od_cw = din("od_cw", [2, 128, 16, 3])
    od_cb = din("od_cb", [2, 128, 16])
    fg_bc = din("fg_bc", [128, D])
    ident_d = din("ident", [128, 128], BF16)
    pmat_d = din("pmat", [128, 128], BF16)
    cos_d = din("cos_t", [128, S_LAT])
    sin_d = din("sin_t", [128, S_LAT])
    mask_d = din("mask_b", [128, 384], BF16)

    if last_is_final:
        out_d = nc.dram_tensor("out", [S_LAT, D], F32, kind="ExternalOutput").ap()
        xo_d = None
    else:
        out_d = None
        xo_d = nc.dram_tensor("xo", [NTOK, D], F32, kind="ExternalOutput").ap()
    multi = len(layer_ids) > 1
    X = nc.dram_tensor("Xs", [NTOK, D], F32).ap() if multi else None
    HT = nc.dram_tensor("HT", [4352, NTOK], BF16).ap()
    VT = nc.dram_tensor("VT", [NTOK, 256], BF16).ap()
    GT = nc.dram_tensor("GT", [2048, NTOK], BF16).ap()

    with ExitStack() as gst:
        P = Prog(nc, gst)

        uid = [0]

        def uname(name):
            uid[0] += 1
            return "%s_%d" % (name, uid[0])

        def gsb(name, shape, dt):
            return gst.enter_context(nc.sbuf_tensor(uname(name), list(shape), dt))

        ident = gsb("ident", [128, 128], BF16)
        pmat = gsb("pmat", [128, 128], BF16)
        onesf = gsb("onesf", [128, 128], F32)
        gs_bc = [gsb("gs_bc%d" % w, [128, D], F32) for w in range(2)]
        sh_bc = [gsb("sh_bc%d" % w, [128, D], F32) for w in range(2)]
        gt_bc = [gsb("gt_bc%d" % w, [128, D], F32) for w in range(2)]

        P.dma("sp", ident[:], ident_d, writes=["ident"], semkey=P.sk("ident"))
        P.dma("sp", pmat[:], pmat_d, writes=["pmat"], semkey=P.sk("pmat"))
        P.memset("dve", onesf[:], 1.0, writes=["onesf"])
        for padeng in ("act", "pool", "dve"):
            if ("pad" + padeng) in DEBUG_SKIP:
                padt = gsb("padt" + padeng, [128, 8], F32)
                for _ in range(9000):
                    P.memset(padeng, padt[:], 0.0) if padeng != "act" else P.op("act", lambda e: e.memzero(padt[:]))
        P.flush()

        if "padpe" in DEBUG_SKIP:
            with ExitStack() as ph:
                padp = ph.enter_context(nc.psum_tensor(uname("padp"), [128, 512], F32))
                for _ in range(700):
                    P.mm(padp[:, :128], ident[:], ident[:], True, True)
                P.flush()
        for li, layer in enumerate(layer_ids):
            even = layer % 2 == 0
            e_idx = layer // 2
            ctx_full = layer <= 1
            ctx_kv = layer == 2
            src = xc if li == 0 else X
            is_final = layer == DEPTH - 1
            if is_final:
                dst = None
            elif li == len(layer_ids) - 1:
                dst = xo_d
            else:
                dst = X
            nconds = 2 if (ctx_full or ctx_kv) else 1

            pwc = [0]

            def wload(tile, dst0, wsrc, c0, c1, tag):
                d = dst0
                for cc in range(c0, c1, 512):
                    P.dma("pool", tile[:, :, d:d + 512], wsrc[:, :, cc:cc + 512],
                          writes=[("W", tag, d // 512)], semkey="PW%d" % (pwc[0] % 16))
                    pwc[0] += 1
                    d += 512

            sc_outer = ExitStack()
            sc_inner = ExitStack()
            if even:
                w_in_v = ev_w_in[e_idx].rearrange("(kc p) n -> p kc n", p=128)
                Wp = sc_inner.enter_context(nc.sbuf_tensor(uname("Wp"), [128, 8, EV_COLS], BF16))
            else:
                w_in_v = od_w_in[e_idx].rearrange("(kc p) n -> p kc n", p=128)
                W1a = sc_outer.enter_context(nc.sbuf_tensor(uname("W1a"), [128, 8, 2048], BF16))
                W0 = sc_inner.enter_context(nc.sbuf_tensor(uname("W0"), [128, 8, 4096], BF16))

            with ExitStack() as ph:
                def sb(name, shape, dt):
                    return ph.enter_context(nc.sbuf_tensor(uname(name), list(shape), dt))

                def psm(name, shape, dt):
                    return ph.enter_context(nc.psum_tensor(uname(name), list(shape), dt))
                ct = sb("a_ct", [128, 8, 2], F32)
                sc = sb("a_sc", [128, 8, 2], F32)
                scb = sb("a_scb", [128, 2, 8, 128], F32)
                awt = [sb("a_awt%d" % i, [128, 8, 512], F32) for i in range(2)]
                abb = sb("a_abb", [128, 3 * D], F32)
                ngb = sb("a_ngb", [128, D], F32)
                tmpa = sb("a_tmp", [128, 512], F32)
                pa = [[psm("a_ps%d_%d" % (w, i), [128, 512], F32) for i in range(2)] for w in range(2)]
                P.dma("sp", ct[:], cT, writes=["ct"], semkey=P.sk("ct"))
                P.dma("sp", abb[:], ada_b_bc[layer], writes=["abb"], semkey=P.sk("abb"))
                P.dma("sp", ngb[:], norm_g_bc[layer], writes=["ngb"], semkey=P.sk("ngb"))
                if even:
                    wload(Wp, 0, w_in_v, 0, EV_COLS, "p")
                else:
                    wload(W0, 0, w_in_v, 2048, 6144, "0")
                P.act(sc[:], ct[:], AF.Silu, reads=["ct"], writes=["sc"])
                for w in range(nconds):
                    for kc in range(8):
                        P.ts("dve", scb[:, w, kc, :], onesf[:], sc[:, kc, w:w + 1], None, ALU.mult,
                             reads=["sc", "onesf"], writes=[("scb", w, kc)])
                awv = ada_w[layer].rearrange("(kc p) n -> p kc n", p=128)
                for n in range(6):
                    s = n % 2
                    P.dma("sp", awt[s][:], awv[:, :, n * 512:(n + 1) * 512], writes=[("awt", s)], semkey=P.sk("awt", s))
                    for w in range(nconds):
                        pp = pa[w][s]
                        for kc in range(8):
                            P.mm(pp[:], scb[:, w, kc, :], awt[s][:, kc, :], kc == 0, kc == 7,
                                 reads=[("awt", s), ("scb", w, kc)], writes=[("pa", w, s)])
                        col = (n % 2) * 512
                        bsl = abb[:, n * 512:(n + 1) * 512]
                        if n < 2:
                            P.tt("dve", sh_bc[w][:, col:col + 512], pp[:], bsl, ALU.add,
                                 reads=[("pa", w, s), "abb"], writes=[("sh", w)])
                        elif n < 4:
                            P.stt(tmpa[:], pp[:], 1.0, bsl, ALU.add, ALU.add,
                                  reads=[("pa", w, s), "abb"], writes=["tmpa"])
                            P.tt("dve", gs_bc[w][:, col:col + 512], tmpa[:], ngb[:, col:col + 512], ALU.mult,
                                 reads=["tmpa", "ngb"], writes=[("gs", w)])
                        else:
                            P.tt("dve", gt_bc[w][:, col:col + 512], pp[:], bsl, ALU.add,
                                 reads=[("pa", w, s), "abb"], writes=[("gt", w)])
                P.flush()

            def projection(pass_id, wt, prefetch=None):
                with ExitStack() as ph:
                    def sb(name, shape, dt):
                        return ph.enter_context(nc.sbuf_tensor(uname(name), list(shape), dt))

                    def psm(name, shape, dt):
                        return ph.enter_context(nc.psum_tensor(uname(name), list(shape), dt))
                    if prefetch is not None:
                        prefetch(sb)
                    xt = [sb("p_xt%d" % i, [128, D], F32) for i in range(3)]
                    tmp = [sb("p_tmp%d" % i, [128, D], F32) for i in range(2)]
                    junk = sb("p_junk", [128, D], BF16)
                    xn = sb("p_xn", [128, 4, D], BF16)
                    xnT = [sb("p_xnT%d" % i, [128, 8, 512], BF16) for i in range(2)]
                    ss = [sb("p_ss%d" % i, [128, 4], F32) for i in range(2)]
                    rstd = [sb("p_rstd%d" % i, [128, 4], F32) for i in range(2)]
                    aux = [sb("p_aux%d" % i, [128, 512], F32) for i in range(2)]
                    stg = [sb("p_stg%d" % i, [128, 512], BF16) for i in range(4)]
                    pst = [psm("p_pst%d" % i, [128, 1024], BF16) for i in range(2)]
                    pm = [psm("p_pm%d" % i, [128, 512], F32) for i in range(6)]
                    if even:
                        cs = [sb("p_cos%d" % i, [128, 512], F32) for i in range(2)]
                        sn = [sb("p_sin%d" % i, [128, 512], F32) for i in range(2)]
                        qb = [sb("p_qb%d" % i, [128, 512], BF16) for i in range(2)]
                        t1 = [sb("p_t1%d" % i, [128, 512], F32) for i in range(2)]
                        t2 = [sb("p_t2%d" % i, [128, 512], F32) for i in range(2)]
                        vst = [sb("p_vst%d" % i, [128, 256], BF16) for i in range(2)]
                    cnt = {"xt": 0, "pm": 0, "stg": 0, "aux": 0, "tmp": 0, "rope": 0, "vst": 0, "ev": 0}

                    def nxt(k, n):
                        v = cnt[k] % n
                        cnt[k] += 1
                        return v

                    def evac_engine():
                        return "act" if nxt("ev", 2) == 0 else "dve"

                    tl = tiles_for(ctx_full or ctx_kv)

                    def front_nonpe(ti):
                        t0, T, cond = tl[ti]
                        nsub = T // 128
                        sl = ti % 2
                        if even and cond == 0:
                            P.dma("sp", cs[sl][:, :T], cos_d[:, t0:t0 + T], writes=[("cos", sl)], semkey=P.sk("cos", sl))
                            P.dma("sp", sn[sl][:, :T], sin_d[:, t0:t0 + T], writes=[("sin", sl)], semkey=P.sk("sin", sl))
                        for j in range(nsub):
                            xs = nxt("xt", 3)
                            P.dma("sp", xt[xs][:], src[t0 + j * 128:t0 + (j + 1) * 128, :], writes=[("xt", xs)], semkey=P.sk("xt", xs))
                            P.act(junk[:], xt[xs][:], AF.Square, reads=[("xt", xs)], writes=["junk", ("ss", sl, j)],
                                  accum_out=ss[sl][:, j:j + 1])
                            P.act(rstd[sl][:, j:j + 1], ss[sl][:, j:j + 1], AF.Sqrt, reads=[("ss", sl, j)], writes=[("rstd", sl, j)],
                                  scale=1.0 / D, bias=EPS)
                            P.recip(rstd[sl][:, j:j + 1], rstd[sl][:, j:j + 1], reads=[("rstd", sl, j)], writes=[("rstd", sl, j)])
                            tm = nxt("tmp", 2)
                            P.stt(tmp[tm][:], xt[xs][:], rstd[sl][:, j:j + 1], gs_bc[cond][:], ALU.mult, ALU.mult,
                                  reads=[("xt", xs), ("rstd", sl, j)], writes=[("tmp", tm)])
                            P.tt("pool", xn[:, j, :], tmp[tm][:], sh_bc[cond][:], ALU.add,
                                 reads=[("tmp", tm)], writes=[("xn", j)])

                    def front_pe(ti):
                        t0, T, cond = tl[ti]
                        nsub = T // 128
                        sl = ti % 2
                        for kc in range(8):
                            h = kc % 2
                            for j in range(nsub):
                                P.tr(pst[h][:, j * 128:(j + 1) * 128], xn[:, j, kc * 128:(kc + 1) * 128], ident[:],
                                     reads=[("xn", j)], writes=[("pst", h)])
                            P.copy(evac_engine(), xnT[sl][:, kc, :T], pst[h][:, :T], reads=[("pst", h)], writes=[("xnT", sl, kc)])

                    def jobs(ti):
                        t0, T, cond = tl[ti]
                        nsub = T // 128
                        sl = ti % 2
                        only_kv = even and cond == 1 and ctx_kv

                        def proj(col0, width=128):
                            W, tag, lc = wt(col0)
                            pi = nxt("pm", 6)
                            wk = [("W", tag, i) for i in range(lc // 512, (lc + width - 1) // 512 + 1)]
                            for kc in range(8):
                                P.mm(pm[pi][:width, :T], W[:, kc, lc:lc + width], xnT[sl][:, kc, :T], kc == 0, kc == 7,
                                     reads=wk + [("xnT", sl, kc)], writes=[("pm", pi)])
                            return pi

                        def store(si, row0):
                            P.dma("sp", HT[row0:row0 + 128, t0:t0 + T], stg[si][:, :T], reads=[("stg", si)], semkey=P.sk("stg", si))

                        if even:
                            if not only_kv:
                                for c in range(8):
                                    pg = proj(1024 + c * 128)
                                    ax = nxt("aux", 2)
                                    P.act(aux[ax][:, :T], pm[pg][:, :T], AF.Sigmoid, reads=[("pm", pg)], writes=[("aux", ax)])
                                    pv = proj(c * 128)
                                    si = nxt("stg", 4)
                                    P.tt("dve", stg[si][:, :T], pm[pv][:, :T], aux[ax][:, :T], ALU.mult,
                                         reads=[("pm", pv), ("aux", ax)], writes=[("stg", si)])
                                    store(si, c * 128)
                            for c in range(10):
                                if only_kv and c < 8:
                                    continue
                                pq = proj(2048 + c * 128)
                                si = nxt("stg", 4)
                                if cond == 1:
                                    P.copy(evac_engine(), stg[si][:, :T], pm[pq][:, :T], reads=[("pm", pq)], writes=[("stg", si)])
                                else:
                                    r = nxt("rope", 2)
                                    P.copy("act", qb[r][:, :T], pm[pq][:, :T], reads=[("pm", pq)], writes=[("qb", r)])
                                    p2 = nxt("pm", 6)
                                    P.mm(pm[p2][:, :T], pmat[:], qb[r][:, :T], True, True, reads=[("qb", r)], writes=[("pm", p2)])
                                    P.tt("dve", t1[r][:, :T], pm[pq][:, :T], cs[sl][:, :T], ALU.mult,
                                         reads=[("pm", pq), ("cos", sl), ("qb", r)], writes=[("t1", r)])
                                    P.tt("dve", t2[r][:, :T], pm[p2][:, :T], sn[sl][:, :T], ALU.mult,
                                         reads=[("pm", p2), ("sin", sl)], writes=[("t2", r)])
                                    P.tt("pool", stg[si][:, :T], t1[r][:, :T], t2[r][:, :T], ALU.add,
                                         reads=[("t1", r), ("t2", r)], writes=[("stg", si)])
                                store(si, 1024 + c * 128)
                            Wv, vtag, vlc = wt(3328)
                            for j in range(nsub):
                                pi = nxt("pm", 6)
                                for kc in range(8):
                                    P.mm(pm[pi][:, :256], xnT[sl][:, kc, j * 128:(j + 1) * 128], Wv[:, kc, vlc:vlc + 256], kc == 0, kc == 7,
                                         reads=[("W", vtag, vlc // 512), ("xnT", sl, kc)], writes=[("pm", pi)])
                                vs = nxt("vst", 2)
                                P.copy(evac_engine(), vst[vs][:], pm[pi][:, :256], reads=[("pm", pi)], writes=[("vst", vs)])
                                P.dma("sp", VT[t0 + j * 128:t0 + (j + 1) * 128, :], vst[vs][:], reads=[("vst", vs)], semkey=P.sk("vst", vs))
                            if not only_kv:
                                for c in range(16):
                                    pg = proj(3584 + c * 128)
                                    si = nxt("stg", 4)
                                    P.act(stg[si][:, :T], pm[pg][:, :T], AF.Silu, reads=[("pm", pg)], writes=[("stg", si)])
                                    store(si, 2304 + c * 128)
                        else:
                            for c in range(16):
                                if pass_id == 0:
                                    p1 = proj(2048 + c * 128)
                                    ax = nxt("aux", 2)
                                    P.copy("act", aux[ax][:, :T], pm[p1][:, :T], reads=[("pm", p1)], writes=[("aux", ax)])
                                    p2 = proj(4096 + c * 128)
                                    row0 = c * 128
                                else:
                                    p1 = proj(6144 + c * 128)
                                    ax = nxt("aux", 2)
                                    P.act(aux[ax][:, :T], pm[p1][:, :T], AF.Silu, reads=[("pm", p1)], writes=[("aux", ax)])
                                    p2 = proj(c * 128)
                                    row0 = 2048 + c * 128
                                si = nxt("stg", 4)
                                P.tt("dve", stg[si][:, :T], pm[p2][:, :T], aux[ax][:, :T], ALU.mult,
                                     reads=[("pm", p2), ("aux", ax)], writes=[("stg", si)])
                                store(si, row0)

                    front_nonpe(0)
                    front_pe(0)
                    for ti in range(len(tl)):
                        if ti + 1 < len(tl):
                            front_nonpe(ti + 1)
                        jobs(ti)
                        if ti + 1 < len(tl):
                            front_pe(ti + 1)
                    P.flush()

            if even:
                projection(0, lambda col: (Wp, "p", col))
                sc_inner.close()
            else:
                def wt0(col):
                    return (W0, "0", col - 2048)

                def pre1(sb):
                    wload(W1a, 0, w_in_v, 6144, 7168, "1a")
                    wload(W1a, 1024, w_in_v, 0, 1024, "1a")
                projection(0, wt0, prefetch=pre1)
                sc_inner.close()
                W1b_box = []

                def pre2(sb):
                    W1b = sb("W1b", [128, 8, 2048], BF16)
                    W1b_box.append(W1b)
                    wload(W1b, 0, w_in_v, 7168, 8192, "1b")
                    wload(W1b, 1024, w_in_v, 1024, 2048, "1b")

                def wt1(col):
                    if col >= 6144:
                        c = col - 6144
                        return (W1a, "1a", c) if c < 1024 else (W1b_box[0], "1b", c - 1024)
                    return (W1a, "1a", 1024 + col) if col < 1024 else (W1b_box[0], "1b", 1024 + col - 1024)
                projection(1, wt1, prefetch=pre2)
            sc_outer.close()
            sc_w = ExitStack()
            wo_v = (ev_w_out if even else od_w_out)[e_idx].rearrange("(c p) n -> p c n", p=128)
            Wo = sc_w.enter_context(nc.sbuf_tensor(uname("Wo"), [128, 16, D], BF16))

            def wo_load():
                for cc in range(0, 16, 2):
                    P.dma("pool", Wo[:, cc:cc + 2, :], wo_v[:, cc:cc + 2, :], writes=[("Wo", cc // 2)], semkey="PW%d" % (pwc[0] % 16))
                    pwc[0] += 1

            if even:
                with ExitStack() as ph:
                    def sb(name, shape, dt):
                        return ph.enter_context(nc.sbuf_tensor(uname(name), list(shape), dt))

                    def psm(name, shape, dt):
                        return ph.enter_context(nc.psum_tensor(uname(name), list(shape), dt))
                    dww = sb("c_dww", [128, 8, 31], F32)
                    vec = sb("c_vec", [128, 3, 8], F32)
                    dg = sb("c_dg", [128, 8, 31, 128], BF16)
                    uh = [sb("c_uh%d" % i, [128, 512 + 30], BF16) for i in range(3)]
                    convt = sb("c_conv", [128, 8, 512], F32)
                    sq = [sb("c_sq%d" % i, [128, 512], F32) for i in range(2)]
                    mean = sb("c_mean", [128, 512], F32)
                    msq = sb("c_msq", [128, 512], F32)
                    rs = sb("c_rs", [128, 512], F32)
                    dd = [sb("c_d%d" % i, [128, 512], F32) for i in range(2)]
                    aa = [sb("c_a%d" % i, [128, 512], BF16) for i in range(2)]
                    sg = [sb("c_sg%d" % i, [128, 512], BF16) for i in range(3)]
                    go = [sb("c_go%d" % i, [128, 512], BF16) for i in range(3)]
                    pc = [psm("c_pc%d" % i, [128, 512], F32) for i in range(3)]
                    psum_s = psm("c_pss", [128, 512], F32)
                    psum_q = psm("c_psq", [128, 512], F32)
                    wo_load()
                    P.dma("sp", dww[:], ev_dww[e_idx], writes=["dww"], semkey=P.sk("dww"))
                    P.dma("sp", vec[:], ev_vec[e_idx], writes=["vec"], semkey=P.sk("vec"))
                    for c in range(8):
                        for k in range(31):
                            P.ts("dve", dg[:, c, k, :], ident[:], dww[:, c, k:k + 1], None, ALU.mult,
                                 reads=["dww"], writes=[("dg", c)])
                    cn = {"uh": 0, "pc": 0, "sq": 0, "d": 0, "sg": 0}

                    def nx(k, n):
                        v = cn[k] % n
                        cn[k] += 1
                        return v
                    for (t0, T, cond) in tiles_for(ctx_full):
                        lo, hi = seg_bounds(cond)
                        a0 = max(t0 - 15, lo)
                        a1 = min(t0 + T + 15, hi)
                        off = a0 - (t0 - 15)
                        for c in range(8):
                            u = nx("uh", 3)
                            if off > 0 or a1 < t0 + T + 15:
                                P.memset("pool", uh[u][:, :T + 30], 0.0, writes=[("uh", u)])
                            P.dma("sp", uh[u][:, off:off + (a1 - a0)], HT[c * 128:(c + 1) * 128, a0:a1], writes=[("uh", u)], semkey=P.sk("uh", u))
                            pi = nx("pc", 3)
                            for k in range(31):
                                P.mm(pc[pi][:, :T], dg[:, c, k, :], uh[u][:, k:k + T], k == 0, k == 30,
                                     reads=[("dg", c), ("uh", u)], writes=[("pc", pi)])
                            P.act(convt[:, c, :T], pc[pi][:, :T], AF.Identity, reads=[("pc", pi), "vec"], writes=[("conv", c)],
                                  bias=vec[:, 0, c:c + 1])
                            q = nx("sq", 2)
                            P.act(sq[q][:, :T], pc[pi][:, :T], AF.Square, reads=[("pc", pi), "vec"], writes=[("sq", q)],
                                  bias=vec[:, 0, c:c + 1])
                            P.mm(psum_s[:, :T], onesf[:], convt[:, c, :T], c == 0, c == 7, reads=[("conv", c)], writes=["pss"])
                            P.mm(psum_q[:, :T], onesf[:], sq[q][:, :T], c == 0, c == 7, reads=[("sq", q)], writes=["psq"])
                        P.ts("dve", mean[:, :T], psum_s[:, :T], 1.0 / D, None, ALU.mult, reads=["pss"], writes=["mean"])
                        P.tt("dve", msq[:, :T], mean[:, :T], mean[:, :T], ALU.mult, reads=["mean"], writes=["msq"])
                        P.stt(rs[:, :T], psum_q[:, :T], 1.0 / D, msq[:, :T], ALU.mult, ALU.subtract, reads=["psq", "msq"], writes=["rs"])
                        P.act(rs[:, :T], rs[:, :T], AF.Sqrt, reads=["rs"], writes=["rs"], bias=EPS)
                        P.recip(rs[:, :T], rs[:, :T], reads=["rs"], writes=["rs"])
                        for c in range(8):
                            s3 = nx("sg", 3)
                            P.dma("sp", sg[s3][:, :T], HT[2304 + c * 128:2304 + (c + 1) * 128, t0:t0 + T], writes=[("sg", s3)], semkey=P.sk("sg", s3))
                            d = nx("d", 2)
                            P.tt("dve", dd[d][:, :T], convt[:, c, :T], mean[:, :T], ALU.subtract, reads=[("conv", c), "mean"], writes=[("d", d)])
                            P.tt("dve", dd[d][:, :T], dd[d][:, :T], rs[:, :T], ALU.mult, reads=[("d", d), "rs"], writes=[("d", d)])
                            P.act(aa[d][:, :T], dd[d][:, :T], AF.Silu, reads=[("d", d), "vec"], writes=[("a", d)],
                                  scale=vec[:, 1, c:c + 1], bias=vec[:, 2, c:c + 1])
                            P.tt("pool", go[s3][:, :T], aa[d][:, :T], sg[s3][:, :T], ALU.mult, reads=[("a", d), ("sg", s3)], writes=[("go", s3)])
                            P.dma("pool", GT[c * 128:(c + 1) * 128, t0:t0 + T], go[s3][:, :T], reads=[("go", s3)], semkey="PS%d" % s3)
                    P.flush()
            else:
                with ExitStack() as ph:
                    def sb(name, shape, dt):
                        return ph.enter_context(nc.sbuf_tensor(uname(name), list(shape), dt))

                    def psm(name, shape, dt):
                        return ph.enter_context(nc.psum_tensor(uname(name), list(shape), dt))
                    cw = sb("o_cw", [128, 16, 3], F32)
                    cb = sb("o_cb", [128, 16], F32)
                    dg = sb("o_dg", [128, 16, 3, 128], BF16)
                    NS3 = 6
                    vh = [sb("o_vh%d" % i, [128, 514], BF16) for i in range(NS3)]
                    wt = [sb("o_wt%d" % i, [128, 512], BF16) for i in range(NS3)]
                    go = [sb("o_go%d" % i, [128, 512], BF16) for i in range(NS3)]
                    pc = [psm("o_pc%d" % i, [128, 512], F32) for i in range(NS3)]
                    wo_load()
                    P.dma("sp", cw[:], od_cw[e_idx], writes=["cw"], semkey=P.sk("cw"))
                    P.dma("sp", cb[:], od_cb[e_idx], writes=["cb"], semkey=P.sk("cb"))
                    for c in range(16):
                        for k in range(3):
                            P.ts("dve", dg[:, c, k, :], ident[:], cw[:, c, k:k + 1], None, ALU.mult,
                                 reads=["cw"], writes=[("dg", c)])
                    cn = 0
                    for (t0, T, cond) in tiles_for(ctx_full):
                        lo, hi = seg_bounds(cond)
                        a0 = max(t0 - 1, lo)
                        a1 = min(t0 + T + 1, hi)
                        off = a0 - (t0 - 1)
                        for c in range(16):
                            u = cn % NS3
                            cn += 1
                            if off > 0 or a1 < t0 + T + 1:
                                P.memset("pool", vh[u][:, :T + 2], 0.0, writes=[("vh", u)])
                            P.dma("sp", vh[u][:, off:off + (a1 - a0)], HT[c * 128:(c + 1) * 128, a0:a1], writes=[("vh", u)], semkey=P.sk("vh", u))
                            P.dma("sp", wt[u][:, :T], HT[2048 + c * 128:2048 + (c + 1) * 128, t0:t0 + T], writes=[("wt", u)], semkey=P.sk("wt", u))
                            for k in range(3):
                                P.mm(pc[u][:, :T], dg[:, c, k, :], vh[u][:, k:k + T], k == 0, k == 2,
                                     reads=[("dg", c), ("vh", u)], writes=[("pc", u)])
                            P.stt(go[u][:, :T], pc[u][:, :T], cb[:, c:c + 1], wt[u][:, :T], ALU.add, ALU.mult,
                                  reads=[("pc", u), ("wt", u), "cb"], writes=[("go", u)])
                            P.dma("pool", GT[c * 128:(c + 1) * 128, t0:t0 + T], go[u][:, :T], reads=[("go", u)], semkey="PS%d" % u)
                    P.flush()

            if even and "att" not in DEBUG_SKIP:
                with ExitStack() as ph:
                    def sb(name, shape, dt):
                        return ph.enter_context(nc.sbuf_tensor(uname(name), list(shape), dt))

                    def psm(name, shape, dt):
                        return ph.enter_context(nc.psum_tensor(uname(name), list(shape), dt))
                    kTd = sb("t_kT", [128, 4, NTOK], BF16)
                    Vt = sb("t_V", [128, 34, 256], BF16)
                    maskb = sb("t_mask", [128, 384], BF16)
                    sinkb = sb("t_sink", [128, 16], F32)
                    sink8 = sb("t_sink8", [128, 16], F32)
                    qt = [sb("t_q%d" % i, [128, 8, 128], BF16) for i in range(2)]
                    sgt = [sb("t_sg%d" % i, [128, 8, 128], BF16) for i in range(2)]
                    gsg = [sb("t_go%d" % i, [128, 8, 128], BF16) for i in range(2)]
                    Pt = [sb("t_P%d" % i, [128, 640], BF16) for i in range(2)]
                    PTs = [sb("t_PT%d" % i, [128, 5, 128], BF16) for i in range(2)]
                    dgr = [sb("t_dg%d" % i, [128, 128], BF16) for i in range(2)]
                    st = [sb("t_st%d" % i, [128, 8], F32) for i in range(4)]
                    Sps = [psm("t_S%d" % i, [128, 2, 512], F32) for i in range(2)]
                    PTp = psm("t_PTp", [128, 2, 512], F32)
                    OTp = [psm("t_OT%d" % i, [128, 512], F32) for i in range(2)]
                    for g in range(4):
                        for hb in range(2):
                            P.dma("sp", kTd[hb * 64:(hb + 1) * 64, g, :], HT[2048 + g * 64:2048 + (g + 1) * 64, :],
                                  writes=[("kT", g, hb)], semkey=P.sk("kT", g, hb))
                    P.dma("sp", Vt[:], VT.rearrange("(j p) d -> p j d", p=128), writes=["V"], semkey=P.sk("V"))
                    P.dma("sp", maskb[:], mask_d, writes=["mask"], semkey=P.sk("mask"))
                    P.dma("sp", sinkb[:], ev_sink_bc[e_idx], writes=["sink"], semkey=P.sk("sink"))
                    P.ts("dve", sink8[:], sinkb[:], 8.0, None, ALU.mult, reads=["sink"], writes=["sink8"])
                    nqb = 34 if ctx_full else 32
                    if "attshort" in DEBUG_SKIP:
                        nqb = 2
                    items = []
                    for qb_ in range(nqb):
                        for qc in range(8):
                            for hb in range(2):
                                items.append((qb_, qc, hb))

                    def geom(qb_):
                        if qb_ < 32:
                            kb0 = max(qb_ - 1, 0)
                            kb1 = min(qb_ + 1, 31)
                            nlb = kb1 - kb0 + 1
                            return kb0, nlb, nlb * 128, kb0 * 128, (kb0 - (qb_ - 1)) * 128
                        return 0, 0, 0, 0, 0

                    def stageA(i):
                        qb_, qc, hb = items[i]
                        s2 = qb_ % 2
                        t0 = qb_ * 128
                        if qc == 0 and hb == 0:
                            P.dma("sp", qt[s2][:], HT[1024:2048, t0:t0 + 128].rearrange("(c p) t -> p c t", p=128),
                                  writes=[("q", s2)], semkey=P.sk("q", s2))
                            P.dma("sp", sgt[s2][:], HT[3328:4352, t0:t0 + 128].rearrange("(c p) t -> p c t", p=128),
                                  writes=[("sg", s2)], semkey=P.sk("sg", s2))
                        kb0, nlb, nl, k0, m0 = geom(qb_)
                        h = 2 * qc + hb
                        kvh = h // 4
                        b0 = hb * 64
                        hs = i % 2
                        s4 = i % 4
                        S = Sps[hs]
                        sv = st[s4]
                        ntot = nl + 256
                        q_ap = qt[s2][b0:b0 + 64, qc, :]
                        kc_ap = kTd[b0:b0 + 64, kvh, S_LAT:NTOK]
                        rk = [("q", s2), ("kT", kvh, hb)]
                        if nl == 0:
                            half, nb = 256, 1
                            P.mm(S[:, 0, :256], q_ap, kc_ap, True, True, reads=rk, writes=[("S", hs)])
                        else:
                            half, nb = ntot // 2, 2
                            P.mm(S[:, 0, :half], q_ap, kTd[b0:b0 + 64, kvh, k0:k0 + half], True, False, reads=rk, writes=[("S", hs)])
                            P.mm(S[:, 0, :half], ident[:], maskb[:, m0:m0 + half], False, True, reads=["mask"], writes=[("S", hs)])
                            tail = nl - half
                            if tail > 0:
                                P.mm(S[:, 1, :tail], q_ap, kTd[b0:b0 + 64, kvh, k0 + half:k0 + nl], True, False, reads=rk, writes=[("S", hs)])
                                P.mm(S[:, 1, :tail], ident[:], maskb[:, m0 + half:m0 + nl], False, True, reads=["mask"], writes=[("S", hs)])
                            P.mm(S[:, 1, tail:tail + 256], q_ap, kc_ap, True, True, reads=rk, writes=[("S", hs)])
                        sin_ap = S[:, 0:nb, :half]
                        P.op("dve", lambda e: e.tensor_reduce(out=sv[:, 0:1], in_=sin_ap, axis=AX.XY, op=ALU.max),
                             reads=[("S", hs)], writes=[("mx", s4)])
                        P.ts("dve", sv[:, 3:4], sv[:, 0:1], sink8[:, h:h + 1], -0.125, ALU.max, ALU.mult,
                             reads=[("mx", s4), "sink8"], writes=[("negm", s4)])

                    def stageB(i):
                        qb_, qc, hb = items[i]
                        kb0, nlb, nl, k0, m0 = geom(qb_)
                        h = 2 * qc + hb
                        hs = i % 2
                        s4 = i % 4
                        S = Sps[hs]
                        sv = st[s4]
                        ntot = nl + 256
                        if nl == 0:
                            half, nb = 256, 1
                        else:
                            half, nb = ntot // 2, 2
                        P.act(Pt[hs][:, :ntot].rearrange("p (b k) -> p b k", k=half), S[:, 0:nb, :half], AF.Exp,
                              reads=[("S", hs), ("negm", s4)], writes=[("P", hs), ("rs", s4)],
                              scale=0.125, bias=sv[:, 3:4], accum_out=sv[:, 4:5])
                        P.act(sv[:, 6:7], sinkb[:, h:h + 1], AF.Exp, reads=["sink", ("negm", s4)], writes=[("psk", s4)],
                              bias=sv[:, 3:4])
                        P.ts("dve", sv[:, 7:8], sv[:, 4:5], sv[:, 6:7], None, ALU.add,
                             reads=[("rs", s4), ("psk", s4)], writes=[("rsum", s4)])
                        P.recip(sv[:, 7:8], sv[:, 7:8], reads=[("rsum", s4)], writes=[("rsum", s4)])
                        P.ts("dve", dgr[hs][:], ident[:], sv[:, 7:8], None, ALU.mult, reads=[("rsum", s4)], writes=[("dgr", hs)])

                    def stageC(i):
                        qb_, qc, hb = items[i]
                        s2 = qb_ % 2
                        t0 = qb_ * 128
                        kb0, nlb, nl, k0, m0 = geom(qb_)
                        h = 2 * qc + hb
                        kvh = h // 4
                        b0 = hb * 64
                        hs = i % 2
                        nblk = nlb + 2
                        for blk in range(nblk):
                            P.mm(PTp[:, blk // 4, (blk % 4) * 128:(blk % 4 + 1) * 128], Pt[hs][:, blk * 128:(blk + 1) * 128], dgr[hs][:], True, True,
                                 reads=[("P", hs), ("dgr", hs)], writes=[("PTp", blk // 4)])
                        n0 = min(nblk, 4)
                        P.copy("act", PTs[hs][:, 0:n0, :], PTp[:, 0, :n0 * 128].rearrange("p (b t) -> p b t", t=128),
                               reads=[("PTp", 0)], writes=[("PTs0", hs)])
                        if nblk > 4:
                            P.copy("dve", PTs[hs][:, 4, :], PTp[:, 1, :128], reads=[("PTp", 1)], writes=[("PTs1", hs)])
                        osl = qc % 2
                        for blk in range(nblk):
                            kt = (kb0 + blk) if blk < nlb else (32 + blk - nlb)
                            P.mm(OTp[osl][b0:b0 + 64, :128], Vt[:, kt, kvh * 64:(kvh + 1) * 64], PTs[hs][:, blk, :], blk == 0, blk == nblk - 1,
                                 reads=["V", ("PTs0", hs)] + ([("PTs1", hs)] if blk >= 4 else []), writes=[("OT", osl, hb)])
                        if hb == 1:
                            P.tt("dve", gsg[s2][:, qc, :], OTp[osl][:, :128], sgt[s2][:, qc, :], ALU.mult,
                                 reads=[("OT", osl, 0), ("OT", osl, 1), ("sg", s2)], writes=[("gsg", s2)])
                            if qc == 7:
                                P.dma("sp", GT[1024:2048, t0:t0 + 128].rearrange("(c p) t -> p c t", p=128), gsg[s2][:],
                                      reads=[("gsg", s2)], semkey=P.sk("gsg", s2))

                    n_it = len(items)
                    for step in range(n_it + 2):
                        if step < n_it:
                            stageA(step)
                        if 0 <= step - 1 < n_it:
                            stageB(step - 1)
                        if 0 <= step - 2 < n_it:
                            stageC(step - 2)
                    P.flush()

            with ExitStack() as ph:
                def sb(name, shape, dt):
                    return ph.enter_context(nc.sbuf_tensor(uname(name), list(shape), dt))

                def psm(name, shape, dt):
                    return ph.enter_context(nc.psum_tensor(uname(name), list(shape), dt))
                gtt = [sb("y_g%d" % i, [128, 16, 512], BF16) for i in range(2)]
                xt = [sb("y_xt%d" % i, [128, D], F32) for i in range(3)]
                tmp = [sb("y_tmp%d" % i, [128, D], F32) for i in range(2)]
                xo = [sb("y_xo%d" % i, [128, D], F32) for i in range(3)]
                py = [psm("y_p%d" % i, [128, 512], F32) for i in range(4)]
                if is_final:
                    fg = sb("y_fg", [128, D], F32)
                    junk = sb("y_junk", [128, D], BF16)
                    sv = [sb("y_sv%d" % i, [128, 2], F32) for i in range(3)]
                    xf = [sb("y_xf%d" % i, [128, D], F32) for i in range(3)]
                    P.dma("sp", fg[:], fg_bc, writes=["fg"], semkey=P.sk("fg"))
                cj = 0
                otl = tiles_for(ctx_full)

                def gtt_load(ti):
                    t0_, T_, _c = otl[ti]
                    P.dma("sp", gtt[ti % 2][:, :, :T_], GT[:, t0_:t0_ + T_].rearrange("(c p) t -> p c t", p=128),
                          writes=[("gtt", ti % 2)], semkey=P.sk("gtt", ti % 2))
                gtt_load(0)
                for ti, (t0, T, cond) in enumerate(otl):
                    g2 = ti % 2
                    if ti + 1 < len(otl):
                        gtt_load(ti + 1)
                    for j in range(T // 128):
                        x3 = cj % 3
                        x2 = cj % 2
                        r0 = t0 + j * 128
                        P.dma("sp", xt[x3][:], src[r0:r0 + 128, :], writes=[("xt", x3)], semkey=P.sk("xt", x3))
                        for nh in range(2):
                            pp = (cj * 2 + nh) % 4
                            for c in range(16):
                                P.mm(py[pp][:], gtt[g2][:, c, j * 128:(j + 1) * 128], Wo[:, c, nh * 512:(nh + 1) * 512], c == 0, c == 15,
                                     reads=[("gtt", g2), ("Wo", c // 2)], writes=[("py", pp)])
                            P.tt("dve", tmp[x2][:, nh * 512:(nh + 1) * 512], py[pp][:], gt_bc[cond][:, nh * 512:(nh + 1) * 512], ALU.mult,
                                 reads=[("py", pp)], writes=[("tmp", x2, nh)])
                        P.tt("pool", xo[x3][:], xt[x3][:], tmp[x2][:], ALU.add,
                             reads=[("xt", x3), ("tmp", x2, 0), ("tmp", x2, 1)], writes=[("xo", x3)])
                        if not is_final:
                            P.dma("pool", dst[r0:r0 + 128, :], xo[x3][:], reads=[("xo", x3)], semkey="PS%d" % x3)
                        else:
                            P.act(junk[:], xo[x3][:], AF.Square, reads=[("xo", x3)], writes=["junk", ("fss", x3)], accum_out=sv[x3][:, 0:1])
                            P.act(sv[x3][:, 1:2], sv[x3][:, 0:1], AF.Sqrt, reads=[("fss", x3)], writes=[("frs", x3)], scale=1.0 / D, bias=EPS)
                            P.recip(sv[x3][:, 1:2], sv[x3][:, 1:2], reads=[("frs", x3)], writes=[("frs", x3)])
                            P.stt(xf[x3][:], xo[x3][:], sv[x3][:, 1:2], fg[:], ALU.mult, ALU.mult,
                                  reads=[("xo", x3), ("frs", x3), "fg"], writes=[("xf", x3)])
                            P.dma("pool", out_d[r0:r0 + 128, :], xf[x3][:], reads=[("xf", x3)], semkey="PS%d" % x3)
                        cj += 1
                P.flush()

            sc_w.close()
            if (not is_final) and li == len(layer_ids) - 1 and not ctx_full:
                with ExitStack() as ph:
                    cpt = ph.enter_context(nc.sbuf_tensor(uname("cp_t"), [128, 2, D], F32))
                    P.dma("sp", cpt[:], src[S_LAT:NTOK, :].rearrange("(j p) n -> p j n", p=128), writes=["cp"], semkey=P.sk("cp"))
                    P.dma("sp", dst[S_LAT:NTOK, :].rearrange("(j p) n -> p j n", p=128), cpt[:], reads=["cp"], semkey=P.sk("cp2"))
                    P.flush()
    return nc


def _consts():
    ident = np.eye(128, dtype=np.float32).astype(ml_dtypes.bfloat16)
    pm = np.zeros((128, 128), np.float32)
    for m in range(128):
        d = m % 64
        partner = m + 16 if (d % 32) < 16 else m - 16
        pm[partner, m] = 1.0
    pm = pm.astype(ml_dtypes.bfloat16)
    t = np.arange(S_LAT)
    row = (t // 64).astype(np.float32)
    col = (t % 64).astype(np.float32)
    freqs = (10000.0 ** (-np.arange(16, dtype=np.float32) / 16)).astype(np.float32)
    cos_t = np.zeros((128, S_LAT), np.float32)
    sin_t = np.zeros((128, S_LAT), np.float32)
    for p in range(128):
        d = p % 64
        pos = row if d < 32 else col
        f = freqs[d % 16]
        ang = (pos * f).astype(np.float32)
        cos_t[p] = np.cos(ang)
        s = np.sin(ang)
        sin_t[p] = -s if (d % 32) < 16 else s
    qi = np.arange(128)[:, None]
    kj = np.arange(384)[None, :]
    rel = kj - qi
    mask = np.where((rel >= 0) & (rel <= 256), 0.0, NEG).astype(np.float32).astype(ml_dtypes.bfloat16)
    return ident, pm, cos_t, sin_t, mask


def _prep_shared(inp):
    f = lambda a: np.ascontiguousarray(np.asarray(a, dtype=np.float32))
    ident, pm, cos_t, sin_t, mask = _consts()
    sh = {}
    sh["ada_w"] = f(inp["ada_w"])
    sh["ada_b_bc"] = f(np.broadcast_to(f(inp["ada_b"])[:, None, :], (DEPTH, 128, 3 * D)))
    sh["norm_g_bc"] = f(np.broadcast_to(f(inp["norm_g"])[:, None, :], (DEPTH, 128, D)))
    sh["ev_w_in"] = f(inp["ev_w_in"])
    sh["ev_w_out"] = f(inp["ev_w_out"])
    sh["od_w_in"] = f(inp["od_w_in"])
    sh["od_w_out"] = f(inp["od_w_out"])
    sh["ev_dww"] = f(f(inp["ev_dw_w"]).reshape(2, 31, 8, 128).transpose(0, 3, 2, 1))
    vec = np.stack([f(inp["ev_dw_b"]), f(inp["ev_ln_g"]), f(inp["ev_ln_b"])], axis=1)
    sh["ev_vec"] = f(vec.reshape(2, 3, 8, 128).transpose(0, 3, 1, 2))
    sh["ev_sink_bc"] = f(np.broadcast_to(f(inp["ev_sink"])[:, None, :], (2, 128, 16)))
    sh["od_cw"] = f(f(inp["od_conv_w"]).reshape(2, 3, 16, 128).transpose(0, 3, 2, 1))
    sh["od_cb"] = f(f(inp["od_conv_b"]).reshape(2, 16, 128).transpose(0, 2, 1))
    sh["fg_bc"] = f(np.broadcast_to(f(inp["final_g"])[None, :], (128, D)))
    sh["ident"] = ident
    sh["pmat"] = pm
    sh["cos_t"] = cos_t
    sh["sin_t"] = sin_t
    sh["mask_b"] = mask
    return sh


def _cT(c_b, c_ctx):
    a = np.stack([np.asarray(c_b, np.float32), np.asarray(c_ctx, np.float32)], axis=-1)
    return np.ascontiguousarray(a.reshape(8, 128, 2).transpose(1, 0, 2))


_NC_CACHE = {}


def _get_nc(layer_ids):
    key = tuple(layer_ids)
    if key not in _NC_CACHE:
        _NC_CACHE[key] = build_nc(list(layer_ids))
    return _NC_CACHE[key]


LAUNCH_PLAN = [[0], [1], [2], [3]]


def kernel(x, c, ctx, c_ctx, norm_g, ada_w, ada_b, ev_w_in, ev_dw_w, ev_dw_b, ev_ln_g, ev_ln_b,
           ev_sink, ev_w_out, od_w_in, od_conv_w, od_conv_b, od_w_out, final_g):
    inp = dict(norm_g=norm_g, ada_w=ada_w, ada_b=ada_b, ev_w_in=ev_w_in, ev_dw_w=ev_dw_w, ev_dw_b=ev_dw_b,
               ev_ln_g=ev_ln_g, ev_ln_b=ev_ln_b, ev_sink=ev_sink, ev_w_out=ev_w_out, od_w_in=od_w_in,
               od_conv_w=od_conv_w, od_conv_b=od_conv_b, od_w_out=od_w_out, final_g=final_g)
    shared = _prep_shared(inp)
    x = np.asarray(x, np.float32)
    ctx = np.asarray(ctx, np.float32)
    c = np.asarray(c, np.float32)
    B = x.shape[0]
    xcs = [np.ascontiguousarray(np.concatenate([x[b], ctx[b]], axis=0)) for b in range(B)]
    cTs = [_cT(c[b], c_ctx) for b in range(B)]
    out = None
    plan = LAUNCH_PLAN
    if os.environ.get("KERNEL_PROBE_PLAN"):
        plan = [[int(c) for c in grp] for grp in os.environ["KERNEL_PROBE_PLAN"].split(",")]
    for layer_ids in plan:
        nc = _get_nc(layer_ids)
        in_maps = []
        for b in range(B):
            m = dict(shared)
            m["xc"] = xcs[b]
            m["cT"] = cTs[b]
            in_maps.append(m)
        res = run_bass_kernel_spmd(nc, in_maps, core_ids=list(range(B)))
        if layer_ids[-1] == DEPTH - 1:
            out = np.stack([np.asarray(res.results[b]["out"], np.float32) for b in range(B)], axis=0)
        else:
            xcs = [np.ascontiguousarray(np.asarray(res.results[b]["xo"], np.float32)) for b in range(B)]
    if out is None:
        if not os.environ.get("KERNEL_PROBE_PLAN"):
            raise RuntimeError("launch plan did not produce the final output")
        out = np.zeros((B, S_LAT, D), np.float32)
    return out
```
